# Optimizing a Trainium2 kernel written in Bass

```python
import math
import jax, jax.numpy as jnp
from jax import lax
import numpy as np

D_MODEL = 2048
BATCH = 8
SEQ = 2048
DEPTH = 2

N_MIXERS = 2
N_POOL_LAYERS = (DEPTH + 1) // 2
N_SSM_LAYERS = DEPTH // 2

ALPHA = (2.0 * DEPTH) ** 0.25
BETA = (8.0 * DEPTH) ** -0.25
LN_EPS = 1e-5

POOL_WINDOWS = (2, 4, 8, 16)
N_POOL_GROUPS = len(POOL_WINDOWS)
POOL_GROUP_DIM = D_MODEL // N_POOL_GROUPS

SSM_EXPAND = 2
D_INNER = SSM_EXPAND * D_MODEL
SSM_HEAD_DIM = 64
SSM_HEADS = D_INNER // SSM_HEAD_DIM
SSM_GROUPS = 8
HEADS_PER_GROUP = SSM_HEADS // SSM_GROUPS
D_STATE = 128
CONV_WIDTH = 4
CHUNK = 128
CONV_DIM = D_INNER + 2 * SSM_GROUPS * D_STATE
D_IN_PROJ = D_INNER + CONV_DIM + SSM_HEADS
RMS_EPS = 1e-5

D_FF = 4 * D_MODEL

PLE_DIM = 256

kernel_name = "pool_ssd_interleaved_deepnorm_hybrid"


def layer_norm(x, g, b):
    xf = x.astype(jnp.float32)
    mu = jnp.mean(xf, axis=-1, keepdims=True)
    var = jnp.mean(jnp.square(xf - mu), axis=-1, keepdims=True)
    y = (xf - mu) * lax.rsqrt(var + LN_EPS) * g.astype(jnp.float32) + b.astype(jnp.float32)
    return y.astype(x.dtype)


def rms_norm(x, g):
    xf = x.astype(jnp.float32)
    y = xf * lax.rsqrt(jnp.mean(jnp.square(xf), axis=-1, keepdims=True) + RMS_EPS)
    return y * g.astype(jnp.float32)


def pool_mixer(x, w, scale):
    bsz, seq, _ = x.shape
    xf = x.astype(jnp.float32)
    cs = jnp.cumsum(xf, axis=1)
    pos = jnp.arange(seq)
    outs = []
    for g, win in enumerate(POOL_WINDOWS):
        sl = slice(g * POOL_GROUP_DIM, (g + 1) * POOL_GROUP_DIM)
        c = cs[..., sl]
        c_prev = jnp.pad(c, ((0, 0), (win, 0), (0, 0)))[:, :seq]
        cnt = jnp.minimum(pos + 1, win).astype(jnp.float32)[:, None]
        outs.append((c - c_prev) / cnt - xf[..., sl])
    pooled = jnp.stack(outs, axis=2).astype(x.dtype)
    y = jnp.einsum('bsgc,gcd->bsgd', pooled, w).reshape(bsz, seq, D_MODEL)
    return y * scale


def causal_depthwise_conv(u, w, b):
    seq = u.shape[1]
    up = jnp.pad(u, ((0, 0), (CONV_WIDTH - 1, 0), (0, 0)))
    out = b
    for k in range(CONV_WIDTH):
        out = out + up[:, k:k + seq] * w[k]
    return out


def ssd_mixer(x, in_w, conv_w, conv_b, dt_bias, a_log, d_skip, norm_w, out_w):
    bsz, seq, _ = x.shape
    nc = seq // CHUNK
    zxbcdt = x @ in_w
    z = zxbcdt[..., :D_INNER]
    xbc = zxbcdt[..., D_INNER:D_INNER + CONV_DIM]
    dt = zxbcdt[..., D_INNER + CONV_DIM:]
    xbc = jax.nn.silu(causal_depthwise_conv(xbc, conv_w, conv_b))
    xs = xbc[..., :D_INNER]
    bm = xbc[..., D_INNER:D_INNER + SSM_GROUPS * D_STATE]
    cm = xbc[..., D_INNER + SSM_GROUPS * D_STATE:]

    dt = jax.nn.softplus(dt.astype(jnp.float32) + dt_bias.astype(jnp.float32))
    a = -jnp.exp(a_log.astype(jnp.float32)).reshape(SSM_GROUPS, HEADS_PER_GROUP)

    xs = xs.astype(jnp.float32).reshape(bsz, nc, CHUNK, SSM_GROUPS, HEADS_PER_GROUP, SSM_HEAD_DIM)
    bm = bm.astype(jnp.float32).reshape(bsz, nc, CHUNK, SSM_GROUPS, D_STATE)
    cm = cm.astype(jnp.float32).reshape(bsz, nc, CHUNK, SSM_GROUPS, D_STATE)
    dt = dt.reshape(bsz, nc, CHUNK, SSM_GROUPS, HEADS_PER_GROUP)

    da = jnp.transpose(dt * a, (0, 3, 4, 1, 2))
    a_cs = jnp.cumsum(da, axis=-1)
    xdt = xs * dt[..., None]

    causal = jnp.tril(jnp.ones((CHUNK, CHUNK), dtype=bool))
    seg = a_cs[..., :, None] - a_cs[..., None, :]
    lmat = jnp.exp(jnp.where(causal, seg, -jnp.inf))
    cb = jnp.einsum('bclgn,bcsgn->bgcls', cm, bm)
    mmat = cb[:, :, None] * lmat
    y_diag = jnp.einsum('bghcls,bcsghp->bclghp', mmat, xdt)

    decay_states = jnp.exp(a_cs[..., -1:] - a_cs)
    xdt_dec = xdt * jnp.transpose(decay_states, (0, 3, 4, 1, 2))[..., None]
    states = jnp.einsum('bclgn,bclghp->bcghpn', bm, xdt_dec)
    chunk_decay = jnp.exp(a_cs[..., -1])

    def step(h, inp):
        s, d = inp
        return d[..., None, None] * h + s, h

    h0 = jnp.zeros((bsz, SSM_GROUPS, HEADS_PER_GROUP, SSM_HEAD_DIM, D_STATE), jnp.float32)
    _, prev = lax.scan(step, h0, (jnp.moveaxis(states, 1, 0), jnp.moveaxis(chunk_decay, 3, 0)))
    prev = jnp.moveaxis(prev, 0, 1)

    state_decay = jnp.transpose(jnp.exp(a_cs), (0, 3, 4, 1, 2))
    y_off = jnp.einsum('bclgn,bcghpn->bclghp', cm, prev) * state_decay[..., None]

    dsk = d_skip.astype(jnp.float32).reshape(SSM_GROUPS, HEADS_PER_GROUP)[..., None]
    y = (y_diag + y_off + xs * dsk).reshape(bsz, seq, D_INNER)
    y = rms_norm(y * jax.nn.silu(z.astype(jnp.float32)), norm_w).astype(x.dtype)
    return y @ out_w


def sq_relu_mlp(x, w1, w2):
    h = jax.nn.relu(x @ w1)
    return (h * h) @ w2


def setup_inputs(seed: int = 0) -> dict:
    key = jax.random.key(seed)
    ks = iter(jax.random.split(key, 32))
    f32 = jnp.float32

    def nrm(shape, scale):
        return jax.random.normal(next(ks), shape, f32) * scale

    x = nrm((BATCH, SEQ, D_MODEL), 1.0)
    p = nrm((DEPTH, BATCH, SEQ, PLE_DIM), 1.0)

    pool_w = nrm((N_POOL_LAYERS, N_POOL_GROUPS, POOL_GROUP_DIM, POOL_GROUP_DIM), BETA * POOL_GROUP_DIM ** -0.5)
    pool_scale = 1.0 + nrm((N_POOL_LAYERS, D_MODEL), 0.1)

    ssm_in_w = nrm((N_SSM_LAYERS, D_MODEL, D_IN_PROJ), D_MODEL ** -0.5)
    ssm_conv_w = nrm((N_SSM_LAYERS, CONV_WIDTH, CONV_DIM), CONV_WIDTH ** -0.5)
    ssm_conv_b = nrm((N_SSM_LAYERS, CONV_DIM), 0.02)
    dt0 = jnp.exp(jax.random.uniform(next(ks), (N_SSM_LAYERS, SSM_HEADS), f32,
                                     math.log(1e-3), math.log(1e-1)))
    ssm_dt_bias = dt0 + jnp.log(-jnp.expm1(-dt0))
    ssm_a_log = jnp.log(jax.random.uniform(next(ks), (N_SSM_LAYERS, SSM_HEADS), f32, 1.0, 16.0))
    ssm_d = 1.0 + nrm((N_SSM_LAYERS, SSM_HEADS), 0.1)
    ssm_norm_w = 1.0 + nrm((N_SSM_LAYERS, D_INNER), 0.1)
    ssm_out_w = nrm((N_SSM_LAYERS, D_INNER, D_MODEL), BETA * D_INNER ** -0.5)

    mlp_w1 = nrm((DEPTH, D_MODEL, D_FF), D_MODEL ** -0.5)
    mlp_w2 = nrm((DEPTH, D_FF, D_MODEL), BETA * D_FF ** -0.5)

    ln_g = 1.0 + nrm((DEPTH, 2, D_MODEL), 0.1)
    ln_b = nrm((DEPTH, 2, D_MODEL), 0.02)

    ple_w = nrm((DEPTH, PLE_DIM, D_MODEL), PLE_DIM ** -0.5)
    ple_gate_w = nrm((DEPTH, D_MODEL, D_MODEL), D_MODEL ** -0.5)

    return {"x": x, "p": p,
            "pool_w": pool_w, "pool_scale": pool_scale,
            "ssm_in_w": ssm_in_w, "ssm_conv_w": ssm_conv_w, "ssm_conv_b": ssm_conv_b,
            "ssm_dt_bias": ssm_dt_bias, "ssm_a_log": ssm_a_log, "ssm_d": ssm_d,
            "ssm_norm_w": ssm_norm_w, "ssm_out_w": ssm_out_w,
            "mlp_w1": mlp_w1, "mlp_w2": mlp_w2,
            "ln_g": ln_g, "ln_b": ln_b,
            "ple_w": ple_w, "ple_gate_w": ple_gate_w}


def reference(x, p, pool_w, pool_scale, ssm_in_w, ssm_conv_w, ssm_conv_b,
              ssm_dt_bias, ssm_a_log, ssm_d, ssm_norm_w, ssm_out_w,
              mlp_w1, mlp_w2, ln_g, ln_b, ple_w, ple_gate_w):
    for i in range(DEPTH):
        j = i // N_MIXERS
        if i % N_MIXERS == 0:
            h = pool_mixer(x, pool_w[j], pool_scale[j])
        else:
            h = ssd_mixer(x, ssm_in_w[j], ssm_conv_w[j], ssm_conv_b[j], ssm_dt_bias[j],
                          ssm_a_log[j], ssm_d[j], ssm_norm_w[j], ssm_out_w[j])
        x = layer_norm(ALPHA * x + h, ln_g[i, 0], ln_b[i, 0])
        h = sq_relu_mlp(x, mlp_w1[i], mlp_w2[i])
        x = layer_norm(ALPHA * x + h, ln_g[i, 1], ln_b[i, 1])
        gate = jax.nn.sigmoid(x @ ple_gate_w[i])
        x = x + gate * (p[i] @ ple_w[i])
    return x
```

```python
import numpy as np
from contextlib import ExitStack
import concourse.bass as bass
import concourse.mybir as mybir
from concourse.bass_utils import run_bass_kernel_spmd

F32 = mybir.dt.float32
BF16 = mybir.dt.bfloat16
ALU = mybir.AluOpType
AF = mybir.ActivationFunctionType
AX = mybir.AxisListType

ENGS = ("pe", "act", "dve", "pool", "sp")
LN_AFF_ENG = "dve"
SSD_SB_ENG = "dve"
SELF_SYNC = {"act", "dve", "pool"}

D = 2048
SEQ = 2048
NT = 512
NPASS = SEQ // NT
HALO = 16
XW = NT + HALO
DFF = 8192
DIN = 4096
NPROJ = 10304
ALPHA = 2.0 ** 0.5
INVA = 1.0 / ALPHA
LN_EPS = 1e-5 / (ALPHA * ALPHA)
RMS_EPS = 1e-5
WINS = (2, 4, 8, 16)

C_LNG = 0
C_LNB = 64
C_PS = 128
C_CW = 144
C_CB = 336
C_NW = 384
NCOL = 416


class Chan:
    def __init__(self, sem):
        self.sem = sem
        self.total = 0


class Prog:
    def __init__(self, nc, es):
        self.nc = nc
        self.es = es
        self.ops = {e: [] for e in ENGS}
        self.lastw = {}
        self.readers = {}
        self.esem = {e: es.enter_context(nc.semaphore("s_" + e)) for e in ENGS}
        self.nchan = 0

    def chan(self):
        self.nchan += 1
        return Chan(self.es.enter_context(self.nc.semaphore("c%d" % self.nchan)))

    def _deps(self, reads, writes):
        deps = []
        for r in reads:
            w = self.lastw.get(r)
            if w is not None:
                deps.append(w)
        for w in writes:
            lw = self.lastw.get(w)
            if lw is not None:
                deps.append(lw)
            deps.extend(self.readers.get(w, {}).values())
        return deps

    def _update(self, reads, writes, tok):
        key = (tok[0], tok[1] if tok[0] == "eng" else id(tok[1]))
        for r in reads:
            self.readers.setdefault(r, {})[key] = tok
        for w in writes:
            self.lastw[w] = tok
            self.readers[w] = {}

    def op(self, eng, fn, reads=(), writes=()):
        deps = self._deps(reads, writes)
        idx = len(self.ops[eng])
        self.ops[eng].append(dict(fn=fn, deps=deps, dma=None))
        self._update(reads, writes, ("eng", eng, idx))

    def dma(self, eng, fn, ch, reads=(), writes=()):
        deps = self._deps(reads, writes)
        if ch.total > 0:
            deps.append(("dma", ch, ch.total))
        ch.total += 16
        self.ops[eng].append(dict(fn=fn, deps=deps, dma=ch))
        self._update(reads, writes, ("dma", ch, ch.total))

    def barrier(self, eng, reads):
        deps = self._deps(reads, ())
        self.ops[eng].append(dict(fn=None, deps=deps, dma=None))

    def emit(self, block):
        for e in ENGS:
            seen = {}
            for i, o in enumerate(self.ops[e]):
                need = {}
                chs = {}
                for d in o["deps"]:
                    if d[0] == "eng":
                        if d[1] == e and (e not in SELF_SYNC and o["dma"] is None):
                            continue
                        k = ("eng", d[1])
                        v = d[2]
                    else:
                        k = ("dma", id(d[1]))
                        v = d[2]
                        chs[k] = d[1]
                    if v > seen.get(k, -1) and v > need.get(k, -1):
                        need[k] = v
                o["need"] = need
                o["chs"] = chs
                for k, v in need.items():
                    seen[k] = v
        sig = {e: set() for e in ENGS}
        for e in ENGS:
            for o in self.ops[e]:
                for k, v in o["need"].items():
                    if k[0] == "eng":
                        sig[k[1]].add(v)
        sigval = {}
        for e in ENGS:
            c = 0
            for i, o in enumerate(self.ops[e]):
                o["sig"] = False
                if i in sig[e]:
                    assert o["dma"] is None and o["fn"] is not None
                    c += 1
                    sigval[(e, i)] = c
                    o["sig"] = True
        self.stats = {e: (len(self.ops[e]), len(sig[e])) for e in ENGS}

        def run(e, h):
            for o in self.ops[e]:
                for k, v in o["need"].items():
                    if k[0] == "eng":
                        h.wait_ge(self.esem[k[1]], sigval[(k[1], v)])
                    else:
                        h.wait_ge(o["chs"][k].sem, v)
                if o["fn"] is None:
                    continue
                inst = o["fn"](h)
                if o["dma"] is not None:
                    inst.then_inc(o["dma"].sem, 16)
                elif o["sig"]:
                    inst.then_inc(self.esem[e], 1)

        @block.tensor
        def _(h):
            run("pe", h)

        @block.scalar
        def _(h):
            run("act", h)

        @block.vector
        def _(h):
            run("dve", h)

        @block.gpsimd
        def _(h):
            run("pool", h)

        @block.sync
        def _(h):
            run("sp", h)


class ABuf:
    def __init__(self, ap, off, nbytes, esize):
        self.ap = ap
        self.off = off
        self.nbytes = nbytes
        self.esize = esize

    def k(self, lo=None, hi=None):
        if lo is None:
            b0, b1 = self.off, self.off + self.nbytes
        else:
            b0, b1 = self.off + lo * self.esize, self.off + hi * self.esize
        return [("ar", b) for b in range(b0 // 1024, (b1 - 1) // 1024 + 1)]


ARENA_BYTES = 97 * 1024


def build_program(n_pass=NPASS, do_l1=True, dbg=None):
    nc = bass.Bass("TRN2", target_bir_lowering=False)

    def din(name, shape):
        return nc.dram_tensor(name, shape, F32, kind="ExternalInput").ap()

    xT_d = din("xT", [D, SEQ])
    pT_d = din("pT", [2, 256, SEQ])
    poolw_d = din("pool_w", [2, 128, 4096])
    inw_d = din("in_w", [8, 5, 128, 4096])
    wdt_d = din("wdt_in", [128, 16 * 64])
    outw_d = din("out_w", [4, 8, 128, 2048])
    w1_d = din("w1", [2, 32, 128, 4096])
    w2_d = din("w2", [2, 4, 16, 128, 2048])
    plew_d = din("ple_w", [2, 256, D])
    gw_d = din("gate_w", [2, 8, 128, 4096])
    colc_d = din("colc", [128, NCOL])
    rowc_d = din("rowc", [128, 3 * 64])
    cmat_d = din("cmat", [128, 4 * 128])
    invc_d = din("invc", [128, 4 * 16])
    outT_d = nc.dram_tensor("outT", [D, SEQ], F32, kind="ExternalOutput").ap()

    es = ExitStack()
    with es:
        P = Prog(nc, es)

        def sb(name, shape, dt=F32):
            return es.enter_context(nc.sbuf_tensor(name, shape, dt))

        xres = sb("xres", [128, 16, XW])
        xbf = sb("xbf", [128, 16, NT], BF16)
        colc = sb("colc_s", [128, NCOL])
        colx = sb("colx", [128, 16])
        rowc = sb("rowc_s", [128, 3, 64])
        arow = sb("arow", [128, 64])
        cmat = sb("cmat_s", [128, 4, 128])
        cmatb = sb("cmatb", [128, 4, 128], BF16)
        invc = sb("invc_s", [128, 4, 16])
        S = sb("S", [128, 8, 512])
        Sbf = sb("Sbf", [128, 512], BF16)
        ctail = sb("ctail", [128, 48, 3])
        ringA = sb("ringA", [128, 3, 16 * 256], BF16)
        ringB = sb("ringB", [128, 3, 4 * 512], BF16)
        wdt = sb("wdt", [128, 16, 64], BF16)
        arena = sb("arena", [128, ARENA_BYTES // 2], BF16)

        psf = [es.enter_context(nc.psum_tensor("ps%d" % i, [128, 512], F32)) for i in range(7)]
        psb = es.enter_context(nc.psum_tensor("psb", [128, 1024], BF16))
        U_f = cmat[:, 0, :]
        SL_f = cmat[:, 1, :]
        ONES_f = cmat[:, 2, :]
        ID_f = cmat[:, 3, :]
        ID_b = cmatb[:, 3, :]

        state = dict(bank=0, aoff=0, ra=0, rb=0)

        def pb():
            i = state["bank"]
            state["bank"] = (i + 1) % 7
            return psf[i], "ps%d" % i

        def areset():
            state["aoff"] = 0

        def aalloc(shape, dt=F32):
            esize = 4 if dt == F32 else 2
            n = int(np.prod(shape))
            nbytes = n * esize
            off = state["aoff"]
            state["aoff"] = (off + nbytes + 63) // 64 * 64
            assert state["aoff"] <= ARENA_BYTES, ("arena overflow", state["aoff"])
            v = arena[:, off // 2:(off + nbytes) // 2]
            if dt == F32:
                v = v.bitcast(F32)
            if len(shape) == 2:
                v = v.rearrange("p (a b) -> p a b", a=shape[0])
            elif len(shape) == 3:
                v = v.rearrange("p (a b c) -> p a b c", a=shape[0], b=shape[1])
            return ABuf(v, off, nbytes, esize)

        chA = [P.chan() for _ in range(3)]
        chB = [P.chan() for _ in range(3)]
        pendA = []
        pendB = []

        def _issueA(srcs):
            i = state["ra"]
            state["ra"] = (i + 1) % 3
            slot = ringA[:, i, :].rearrange("p (k n) -> p k n", k=16)
            key = "ringA%d" % i
            flat = ringA[:, i, :]
            P.dma("pool", lambda h: h.dma_start(out=flat, in_=srcs), chA[i], writes=[key])
            return slot, key

        def prefetchA(tag, srcs):
            pendA.append((tag, _issueA(srcs)))

        def loadA(tag, srcs):
            if pendA and pendA[0][0] == tag:
                return pendA.pop(0)[1]
            assert not pendA, (tag, pendA[0][0])
            return _issueA(srcs)

        def _issueB(src):
            i = state["rb"]
            state["rb"] = (i + 1) % 3
            slot = ringB[:, i, :].rearrange("p (j n) -> p j n", j=4)
            key = "ringB%d" % i
            flat = ringB[:, i, :]
            P.dma("pool", lambda h: h.dma_start(out=flat, in_=src), chB[i], writes=[key])
            return slot, key

        def prefetchB(tag, src):
            pendB.append((tag, _issueB(src)))

        def loadB(tag, src):
            if pendB and pendB[0][0] == tag:
                return pendB.pop(0)[1]
            assert not pendB, (tag, pendB[0][0])
            return _issueB(src)

        def MM(out, lhsT, rhs, st, sp, R, W):
            P.op("pe", lambda h: h.matmul(out, lhsT=lhsT, rhs=rhs, start=st, stop=sp), R, W)

        def TR(out, in_, ident, R, W):
            P.op("pe", lambda h: h.transpose(out, in_, ident), R, W)

        def ACT(out, in_, func, R, W, bias=None, scale=None, accum=None):
            kw = {}
            if bias is not None:
                kw["bias"] = bias
            if scale is not None:
                kw["scale"] = scale
            if accum is not None:
                kw["accum_out"] = accum
            P.op("act", lambda h: h.activation(out=out, in_=in_, func=func, **kw), R, W)

        def TT(eng, out, in0, in1, op, R, W):
            P.op(eng, lambda h: h.tensor_tensor(out=out, in0=in0, in1=in1, op=op), R, W)

        def TS(eng, out, in0, s1, s2, op0, op1, R, W):
            P.op(eng, lambda h: h.tensor_scalar(out=out, in0=in0, scalar1=s1, scalar2=s2, op0=op0, op1=op1), R, W)

        def STT(eng, out, in0, sc, in1, op0, op1, R, W):
            P.op(eng, lambda h: h.scalar_tensor_tensor(out=out, in0=in0, scalar=sc, in1=in1, op0=op0, op1=op1), R, W)

        def CP(eng, out, in_, R, W):
            if eng == "act":
                P.op("act", lambda h: h.copy(out=out, in_=in_), R, W)
            else:
                P.op(eng, lambda h: h.tensor_copy(out=out, in_=in_), R, W)

        def MSET(eng, ap, val, W):
            P.op(eng, lambda h: h.memset(ap, val), (), W)

        def xk(m):
            return "xres%d" % m

        def bk(m):
            return "xbf%d" % m

        XK = [xk(m) for m in range(16)]
        BK = [bk(m) for m in range(16)]

        chc = P.chan()
        for (dst, src, key) in ((colc[:], colc_d, "colc"), (rowc[:].rearrange("p a b -> p (a b)"), rowc_d, "rowc"),
                                (cmat[:].rearrange("p a b -> p (a b)"), cmat_d, "cmat"),
                                (invc[:].rearrange("p a b -> p (a b)"), invc_d, "invc")):
            P.dma("sp", (lambda dst, src: lambda h: h.dma_start(out=dst, in_=src))(dst, src), chc, writes=[key])
        chw = P.chan()
        P.dma("pool", lambda h: h.dma_start(out=wdt[:].rearrange("p k n -> p (k n)"), in_=wdt_d), chw, writes=["wdt"])
        CP("dve", cmatb[:], cmat[:], ["cmat"], ["cmatb"])
        P.op("dve", lambda h: h.tensor_scalar_mul(out=colx[:], in0=colc[:, C_PS:C_PS + 16], scalar1=INVA), ["colc"], ["colx"])
        ACT(arow[:], rowc[:, 1, :], AF.Exp, ["rowc"], ["arow"])
        P.op("dve", lambda h: h.tensor_scalar_mul(out=arow[:], in0=arow[:], scalar1=-1.0), ["arow"], ["arow"])
        MSET("dve", S[:], 0.0, ["S%d" % g for g in range(8)])
        MSET("dve", ctail[:], 0.0, ["ctail%d" % i for i in range(48)])
        MSET("dve", xres[:, :, 0:HALO], 0.0, XK)

        chx = [P.chan() for _ in range(4)]
        cho = [P.chan() for _ in range(4)]
        chp = P.chan()
        chpw = P.chan()
        xT_v = xT_d.rearrange("(c p) t -> p c t", p=128)
        outT_v = outT_d.rearrange("(c p) t -> p c t", p=128)

        def store_out(t0):
            for i in range(4):
                src = xres[:, 4 * i:4 * i + 4, HALO:]
                dst = outT_v[:, 4 * i:4 * i + 4, t0:t0 + NT]
                P.dma("sp", (lambda dst, src: lambda h: h.dma_start(out=dst, in_=src))(dst, src), cho[i],
                      reads=XK[4 * i:4 * i + 4], writes=["out%d" % i])

        ONES_b = cmatb[:, 2, :]

        def layer_norm(li):
            areset()
            zb = aalloc([2, NT], BF16)
            sq = aalloc([2, NT], BF16)
            mean = aalloc([NT])
            msq = aalloc([NT])
            rstd = aalloc([NT])
            nmr = aalloc([NT])
            tt = aalloc([16, NT])
            b1, k1 = pb()
            b2, k2 = pb()
            for m in range(16):
                xm = xres[:, m, HALO:]
                zk = zb.k((m % 2) * NT, (m % 2 + 1) * NT)
                sk = sq.k((m % 2) * NT, (m % 2 + 1) * NT)
                CP("dve", zb.ap[:, m % 2, :], xm, [xk(m)], zk)
                ACT(sq.ap[:, m % 2, :], xm, AF.Square, [xk(m)], sk)
                MM(b1[:], ONES_b, zb.ap[:, m % 2, :], m == 0, m == 15, zk + ["cmatb"], [k1])
                MM(b2[:], ONES_b, sq.ap[:, m % 2, :], m == 0, m == 15, sk + ["cmatb"], [k2])
            P.op("act", lambda h: h.mul(out=mean.ap, in_=b1[:], mul=1.0 / D), [k1], mean.k())
            TT("dve", msq.ap, mean.ap, mean.ap, ALU.mult, mean.k(), msq.k())
            STT("dve", rstd.ap, b2[:], 1.0 / D, msq.ap, ALU.mult, ALU.subtract, [k2] + msq.k(), rstd.k())
            ACT(rstd.ap, rstd.ap, AF.Ln, rstd.k(), rstd.k(), bias=LN_EPS, scale=1.0)
            ACT(rstd.ap, rstd.ap, AF.Exp, rstd.k(), rstd.k(), scale=-0.5)
            def ln_sub(p):
                m0 = 2 * p
                t2k = tt.k(m0 * NT, (m0 + 2) * NT)
                TT("dve", tt.ap[:, m0:m0 + 2, :], xres[:, m0:m0 + 2, HALO:], mean.ap.unsqueeze(1).broadcast_to([128, 2, NT]), ALU.subtract,
                   [xk(m0), xk(m0 + 1)] + mean.k(), t2k)

            def ln_mul(p):
                m0 = 2 * p
                t2 = tt.ap[:, m0:m0 + 2, :]
                t2k = tt.k(m0 * NT, (m0 + 2) * NT)
                TT("dve", t2, t2, rstd.ap.unsqueeze(1).broadcast_to([128, 2, NT]), ALU.mult, t2k + rstd.k(), t2k)
                for m in (m0, m0 + 1):
                    g_ap = colc[:, C_LNG + li * 16 + m:C_LNG + li * 16 + m + 1]
                    b_ap = colc[:, C_LNB + li * 16 + m:C_LNB + li * 16 + m + 1]
                    ACT(xbf[:, m, :], tt.ap[:, m, :], AF.Identity, tt.k(m * NT, (m + 1) * NT) + ["colc"], [bk(m)], bias=b_ap, scale=g_ap)

            ln_sub(0)
            ln_sub(1)
            for p in range(8):
                ln_mul(p)
                if p + 2 < 8:
                    ln_sub(p + 2)
            for m in range(16):
                g_ap = colc[:, C_LNG + li * 16 + m:C_LNG + li * 16 + m + 1]
                b_ap = colc[:, C_LNB + li * 16 + m:C_LNB + li * 16 + m + 1]
                TS(LN_AFF_ENG, xres[:, m, HALO:], tt.ap[:, m, :], g_ap, b_ap, ALU.mult, ALU.add, tt.k(m * NT, (m + 1) * NT) + ["colc"], [xk(m)])

        def w1_src(i, s):
            return w1_d[i, s]

        def mlp_prefetch(i):
            for s in range(3):
                prefetchA(("w1", i, s), w1_src(i, s))

        def mlp(i):
            areset()
            hT = aalloc([64, NT], BF16)
            rtmp = aalloc([2, NT])
            def evac1(j, b, kb):
                r = rtmp.ap[:, j % 2, :]
                rk = rtmp.k((j % 2) * NT, (j % 2 + 1) * NT)
                ACT(r, b[:], AF.Relu, [kb], rk)
                TT("dve", hT.ap[:, j, :], r, r, ALU.mult, rk, hT.k(j * NT, (j + 1) * NT))

            slabs01 = [loadA(("w1", i, s), w1_src(i, s)) for s in range(2)]
            banks01 = [pb() for _ in range(4)]
            for k in range(16):
                for jj in range(4):
                    sl3, skey = slabs01[jj // 2]
                    MM(banks01[jj][0][:], sl3[:, k, (jj % 2) * 128:(jj % 2 + 1) * 128], xbf[:, k, :], k == 0, k == 15,
                       [skey, bk(k)], [banks01[jj][1]])
            for jj in range(4):
                evac1(jj, banks01[jj][0], banks01[jj][1])
            for s in range(2, 32):
                sl3, skey = loadA(("w1", i, s), w1_src(i, s))
                for jl in range(2):
                    j = 2 * s + jl
                    b, kb = pb()
                    for k in range(16):
                        MM(b[:], sl3[:, k, jl * 128:(jl + 1) * 128], xbf[:, k, :], k == 0, k == 15, [skey, bk(k)], [kb])
                    evac1(j, b, kb)
            for mg in range(4):
                banks = [pb() for _ in range(4)]
                for jb in range(16):
                    unit, ukey = loadB(("w2", i, mg, jb), w2_d[i, mg, jb])
                    for jl in range(4):
                        j = 4 * jb + jl
                        for ml in range(4):
                            MM(banks[ml][0][:], unit[:, jl, ml * 128:(ml + 1) * 128], hT.ap[:, j, :], j == 0, j == 63,
                               [ukey] + hT.k(j * NT, (j + 1) * NT), [banks[ml][1]])
                for ml in range(4):
                    m = 4 * mg + ml
                    xm = xres[:, m, HALO:]
                    STT("dve", xm, banks[ml][0][:], INVA, xm, ALU.mult, ALU.add, [banks[ml][1], xk(m)], [xk(m)])

        def gw_src(i, s):
            return gw_d[i, s]

        def ple_prefetch(i):
            for s in range(3):
                prefetchA(("gw", i, s), gw_src(i, s))

        def ple(i, t0, extra=None):
            areset()
            plw = aalloc([2, D], BF16)
            sig = aalloc([2, NT])
            tmp = aalloc([2, NT])
            pTs = aalloc([2, NT], BF16)
            P.dma("pool", lambda h: h.dma_start(out=pTs.ap, in_=pT_d[i].rearrange("(k p) t -> p k t", p=128)[:, :, t0:t0 + NT]), chp, writes=pTs.k())
            P.dma("pool", lambda h: h.dma_start(out=plw.ap, in_=plew_d[i].rearrange("(k p) n -> p k n", p=128)), chpw, writes=plw.k())
            def tail(m, bg, kg):
                bp, kp = pb()
                for k in range(2):
                    MM(bp[:], plw.ap[:, k, m * 128:(m + 1) * 128], pTs.ap[:, k, :], k == 0, k == 1, plw.k() + pTs.k(), [kp])
                sg = sig.ap[:, m % 2, :]
                sk = sig.k((m % 2) * NT, (m % 2 + 1) * NT)
                tp = tmp.ap[:, m % 2, :]
                tk = tmp.k((m % 2) * NT, (m % 2 + 1) * NT)
                ACT(sg, bg[:], AF.Sigmoid, [kg], sk)
                TT("dve", tp, sg, bp[:], ALU.mult, sk + [kp], tk)
                xm = xres[:, m, HALO:]
                TT("dve", xm, xm, tp, ALU.add, [xk(m)] + tk, [xk(m)])
                if extra is not None and m == 0:
                    extra["th"] = extra["start"]()
                if extra is not None and extra["th"]:
                    extra["th"].pop(0)()

            slabs01 = [loadA(("gw", i, s), gw_src(i, s)) for s in range(2)]
            banks01 = [pb() for _ in range(4)]
            for k in range(16):
                for mm in range(4):
                    sl3, skey = slabs01[mm // 2]
                    MM(banks01[mm][0][:], sl3[:, k, (mm % 2) * 128:(mm % 2 + 1) * 128], xbf[:, k, :], k == 0, k == 15,
                       [skey, bk(k)], [banks01[mm][1]])
            for mm in range(4):
                tail(mm, banks01[mm][0], banks01[mm][1])
            for s in range(2, 8):
                sl3, skey = loadA(("gw", i, s), gw_src(i, s))
                for ml in range(2):
                    m = 2 * s + ml
                    bg, kg = pb()
                    for k in range(16):
                        MM(bg[:], sl3[:, k, ml * 128:(ml + 1) * 128], xbf[:, k, :], k == 0, k == 15, [skey, bk(k)], [kg])
                    tail(m, bg, kg)
            if extra is not None:
                while extra["th"]:
                    extra["th"].pop(0)()

        def pw_src(hf):
            return poolw_d[hf]

        def pool_prefetch():
            for hf in range(2):
                prefetchA(("pw", hf), pw_src(hf))

        def pool_front(q, X16, Xkeys, fresh):
            if fresh:
                areset()
            engs = ("dve", "pool") if q == 0 else ("dve",)
            sbufs = {e: (aalloc([4, XW]), aalloc([4, XW])) for e in engs}
            pooled = aalloc([16, NT], BF16)
            fix = {e: aalloc([4, 16]) for e in engs} if q == 0 else None
            th = []
            for g in (3, 2, 1, 0):
                eng = "dve" if (g in (3, 0) or q > 0) else "pool"
                sA, sB = sbufs[eng]
                X = X16[:, 4 * g:4 * g + 4, :]
                Xk = Xkeys(4 * g, 4 * g + 4)
                win = WINS[g]
                cur, curk = X, Xk
                sh = 1
                dst = sA
                while sh < win:
                    lo = 2 * sh - 1
                    th.append((lambda eng, dst, cur, curk, lo, sh: lambda: TT(eng, dst.ap[:, :, lo:], cur[:, :, lo:], cur[:, :, lo - sh:XW - sh],
                                                                             ALU.add, curk, dst.k()))(eng, dst, cur, curk, lo, sh))
                    cur, curk = dst.ap, dst.k()
                    dst = sB if dst is sA else sA
                    sh *= 2
                pk = pooled.k(4 * g * NT, (4 * g + 4) * NT)
                th.append((lambda g, cur, curk, X, Xk, win, pk: lambda: STT("dve", pooled.ap[:, 4 * g:4 * g + 4, :], cur[:, :, HALO:], 1.0 / win,
                                                                           X[:, :, HALO:], ALU.mult, ALU.subtract, curk + Xk, pk))(g, cur, curk, X, Xk, win, pk))
                if q == 0:
                    fx = fix[eng]
                    th.append((lambda g, fx, cur, curk: lambda: TT("dve", fx.ap, cur[:, :, HALO:HALO + 16],
                                                                  invc[:, g:g + 1, :].broadcast_to([128, 4, 16]), ALU.mult,
                                                                  curk + ["invc"], fx.k()))(g, fx, cur, curk))
                    th.append((lambda g, fx, X, Xk, pk: lambda: TT("dve", pooled.ap[:, 4 * g:4 * g + 4, 0:16], fx.ap, X[:, :, HALO:HALO + 16],
                                                                  ALU.subtract, fx.k() + Xk, pk))(g, fx, X, Xk, pk))
            return pooled, th

        def pool_back(pooled, X16, Xkeys):
            slabs = [loadA(("pw", hf), pw_src(hf)) for hf in range(2)]
            for m in range(16):
                g = m // 4
                ml = m % 4
                sl3, skey = slabs[ml // 2]
                b, kb = pb()
                for k in range(4):
                    MM(b[:], sl3[:, 4 * g + k, (ml % 2) * 128:(ml % 2 + 1) * 128], pooled.ap[:, 4 * g + k, :], k == 0, k == 3,
                       [skey] + pooled.k((4 * g + k) * NT, (4 * g + k + 1) * NT), [kb])
                xm = xres[:, m, HALO:]
                src_ok = Xkeys(m, m + 1)
                STT("dve", xm, b[:], colx[:, m:m + 1], X16[:, m, HALO:], ALU.mult, ALU.add, [kb, "colx"] + src_ok, [xk(m)])

        def ssd_srcs(g):
            return [(("inx", g, 0), inw_d[g, 0]), (("inx", g, 1), inw_d[g, 1]), (("inbc", g), inw_d[g, 2]),
                    (("inz", g, 0), inw_d[g, 3]), (("inz", g, 1), inw_d[g, 4])]

        def ow_src(mg, ib):
            return outw_d[mg, ib]

        def ssd_prefetch(g):
            for tag, srcs in ssd_srcs(g)[:3]:
                prefetchA(tag, srcs)

        def ssd_mixer(q):
            areset()
            ygT = aalloc([32, NT], BF16)
            sz = aalloc([4, NT])
            cacc = aalloc([4, NT])
            accr = aalloc([2, NT])
            ubuf = aalloc([2, 528])
            BT = aalloc([NT], BF16)
            CT = aalloc([NT], BF16)
            Btm = aalloc([4, 128], BF16)
            Rb = aalloc([2, 8, 128], BF16)
            Lb = aalloc([2, 8, 128])
            MT = aalloc([2, 8, 128], BF16)
            xdt = aalloc([2, NT], BF16)
            xdd = aalloc([2, NT], BF16)
            xsD = aalloc([2, NT])
            CBm = aalloc([2, 128])
            t1 = aalloc([2, NT])
            DT = aalloc([4, 64]); DA = aalloc([4, 64]); SD = aalloc([4, 64]); DTD = aalloc([4, 64]); CDc = aalloc([4, 64])
            SSQ = aalloc([4, 8])
            scratch0 = state["aoff"]
            V = aalloc([4, 64]); Mx = aalloc([4, 64]); Tn = aalloc([4, 64])
            ACS = Mx
            TOT = Tn
            DEC = V
            state["aoff"] = scratch0
            junk = aalloc([NT], BF16)
            scratch1 = state["aoff"]
            DAH = aalloc([4, 64]); DAL = aalloc([4, 64]); DAHb = aalloc([4, 64], BF16)
            state["aoff"] = scratch1
            RS = aalloc([16])
            Dg = aalloc([128])
            RBc = aalloc([NT])
            tb2 = t1

            for m in range(16):
                CP("act" if m % 2 else "dve", xbf[:, m, :], xres[:, m, HALO:], [xk(m)], [bk(m)])
            P.op("dve", lambda h: h.memset(SSQ.ap, 0.0), (), SSQ.k())

            for c in range(4):
                b, kb = pb()
                for k in range(16):
                    MM(b[:, 0:64], xbf[:, k, c * 128:(c + 1) * 128], wdt[:, k, :], k == 0, k == 15, [bk(k), "wdt"], [kb])
                TT("dve", V.ap[:, c, :], b[:, 0:64], rowc[:, 0, :], ALU.add, [kb, "rowc"], V.k())
            P.op("dve", lambda h: h.tensor_scalar_max(out=Mx.ap, in0=V.ap, scalar1=0.0), V.k(), Mx.k())
            STT("dve", Tn.ap, Mx.ap, -2.0, V.ap, ALU.mult, ALU.add, Mx.k() + V.k(), Tn.k())
            ACT(Tn.ap, Tn.ap, AF.Exp, Tn.k(), Tn.k())
            ACT(Tn.ap, Tn.ap, AF.Ln, Tn.k(), Tn.k(), bias=1.0, scale=1.0)
            TT("dve", DT.ap, Mx.ap, Tn.ap, ALU.add, Mx.k() + Tn.k(), DT.k())
            TT("dve", DA.ap, DT.ap, arow[:].unsqueeze(1).broadcast_to([128, 4, 64]), ALU.mult, DT.k() + ["arow"], DA.k())
            def dt_part2():
                for c in range(4):
                    b, kb = pb()
                    MM(b[:, 0:64], U_f, DA.ap[:, c, :], True, True, DA.k() + ["cmat"], [kb])
                    MM(b[:, 64:128], ONES_f, DA.ap[:, c, :], True, True, DA.k() + ["cmat"], [kb])
                    CP("act", ACS.ap[:, c, :], b[:, 0:64], [kb], ACS.k())
                    CP("act", TOT.ap[:, c, :], b[:, 64:128], [kb], TOT.k())
                ACT(SD.ap, ACS.ap, AF.Exp, ACS.k(), SD.k())
                TT("dve", DEC.ap, TOT.ap, ACS.ap, ALU.subtract, TOT.k() + ACS.k(), DEC.k())
                ACT(DEC.ap, DEC.ap, AF.Exp, DEC.k(), DEC.k())
                TT("dve", DTD.ap, DT.ap, DEC.ap, ALU.mult, DT.k() + DEC.k(), DTD.k())
                ACT(CDc.ap, TOT.ap, AF.Exp, TOT.k(), CDc.k())
                CP("dve", DAHb.ap, DA.ap, DA.k(), DAHb.k())
                CP("dve", DAH.ap, DAHb.ap, DAHb.k(), DAH.k())
                TT("dve", DAL.ap, DA.ap, DAH.ap, ALU.subtract, DA.k() + DAH.k(), DAL.k())


            def bc8(buf, c, g, n):
                return buf.ap[:, c, 8 * g:8 * g + 8].unsqueeze(2).broadcast_to([128, 8, n])

            def rbuild(g, c):
                U_b = cmatb[:, 0, :]
                for hq in range(2):
                    for w, src in ((0, DAH), (1, DAL)):
                        TT("dve", Rb.ap[:, w, 4 * hq:4 * hq + 4, :], U_b.unsqueeze(1).broadcast_to([128, 4, 128]),
                           src.ap[:, c, 8 * g + 4 * hq:8 * g + 4 * hq + 4].unsqueeze(2).broadcast_to([128, 4, 128]), ALU.mult,
                           ["cmatb"] + src.k(), Rb.k(w * 1024 + hq * 512, w * 1024 + (hq + 1) * 512))

            slabsG = {}

            def getG(gg, idx):
                if (gg, idx) not in slabsG:
                    sr = ssd_srcs(gg)
                    slabsG[(gg, idx)] = loadA(sr[idx][0], sr[idx][1])
                return slabsG[(gg, idx)]

            def inproj_parts(gg, i):
                b, kb = pb()
                if i < 4:
                    sl3, skey = getG(gg, i // 2)
                    c0 = (i % 2) * 128
                else:
                    sl3, skey = getG(gg, 2)
                    c0 = (i - 4) * 128
                for k in range(16):
                    MM(b[:], sl3[:, k, c0:c0 + 128], xbf[:, k, :], k == 0, k == 15, [skey, bk(k)], [kb])
                cc = (gg * 4 + i) if i < 4 else (32 + gg if i == 4 else 40 + gg)
                ub = ubuf.ap[:, i % 2, :]
                ubk = ubuf.k((i % 2) * 528, (i % 2 + 1) * 528)
                tk = "ctail%d" % (gg * 6 + i)
                if i < 4:
                    acc = cacc.ap[:, i, :]
                    ak = cacc.k(i * NT, (i + 1) * NT)
                else:
                    acc = accr.ap[:, i % 2, :]
                    ak = accr.k((i % 2) * NT, (i % 2 + 1) * NT)

                def wcol(kk):
                    return colc[:, C_CW + kk * 48 + cc:C_CW + kk * 48 + cc + 1]

                def fin():
                    if i < 4:
                        ACT(acc, acc, AF.Silu, ak, ak)
                    elif i == 4:
                        ACT(BT.ap, acc, AF.Silu, ak, BT.k())
                    else:
                        ACT(CT.ap, acc, AF.Silu, ak, CT.k())
                return [
                    lambda: CP("act", ub[:, 3:3 + NT], b[:], [kb], ubk),
                    lambda: CP("dve", ub[:, 0:3], ctail[:, gg * 6 + i, :], [tk], ubk),
                    lambda: CP("dve", ctail[:, gg * 6 + i, :], ub[:, NT:NT + 3], ubk, [tk]),
                    lambda: TS("dve", acc, ub[:, 0:NT], wcol(0), colc[:, C_CB + cc:C_CB + cc + 1], ALU.mult, ALU.add, ubk + ["colc"], ak),
                    lambda: STT("dve", acc, ub[:, 1:1 + NT], wcol(1), acc, ALU.mult, ALU.add, ubk + ak + ["colc"], ak),
                    lambda: STT("dve", acc, ub[:, 2:2 + NT], wcol(2), acc, ALU.mult, ALU.add, ubk + ak + ["colc"], ak),
                    lambda: STT("dve", acc, ub[:, 3:3 + NT], wcol(3), acc, ALU.mult, ALU.add, ubk + ak + ["colc"], ak),
                    fin,
                ]

            carry = {}
            cnt = 0
            for g in range(8):
                Sk = "S%d" % g
                CP("act", Sbf[:], S[:, g, :], [Sk], ["Sbf"])

                def get(idx, g=g):
                    return getG(g, idx)
                for i0 in (0, 2, 4):
                    if (g, i0) in carry:
                        pa, pb_ = carry.pop((g, i0))
                    else:
                        pa, pb_ = inproj_parts(g, i0), inproj_parts(g, i0 + 1)
                    for fa, fb in zip(pa, pb_):
                        fa()
                        fb()
                zb = [pb() for _ in range(4)]
                for hf in range(2):
                    sl3, skey = get(3 + hf)
                    for c in range(4):
                        for k in range(16):
                            MM(zb[c][0][:, hf * 256:(hf + 1) * 256], xbf[:, k, c * 128:(c + 1) * 128], sl3[:, k, :], k == 0, k == 15,
                               [skey, bk(k)], [zb[c][1]])
                for c in range(4):
                    ACT(sz.ap[:, c, :], zb[c][0][:], AF.Silu, [zb[c][1]], sz.k(c * NT, (c + 1) * NT))
                if g == 0:
                    dt_part2()
                if g < 7:
                    ssd_prefetch(g + 1)
                else:
                    for ib in range(3):
                        prefetchB(("ow", 0, ib), ow_src(0, ib))
                for c in range(4):
                    TR(psb[:, c * 128:(c + 1) * 128], BT.ap[:, c * 128:(c + 1) * 128], ID_b, BT.k() + ["cmatb"], ["psb"])
                CP("act", Btm.ap, psb[:, 0:512].rearrange("p (c n) -> p c n", c=4), ["psb"], Btm.k())

                def stageA(c, pr):
                    cs = slice(c * 128, (c + 1) * 128)
                    xdt_a, xdt_k = xdt.ap[:, pr, :], xdt.k(pr * NT, (pr + 1) * NT)
                    xdd_a, xdd_k = xdd.ap[:, pr, :], xdd.k(pr * NT, (pr + 1) * NT)
                    xsD_a, xsD_k = xsD.ap[:, pr, :], xsD.k(pr * NT, (pr + 1) * NT)
                    CBm_a, CBm_k = CBm.ap[:, pr, :], CBm.k(pr * 128, (pr + 1) * 128)
                    Lb_a = [Lb.ap[:, pr, 4 * hq:4 * hq + 4] for hq in range(2)]
                    Lb_k = [Lb.k(pr * 1024 + hq * 512, pr * 1024 + (hq + 1) * 512) for hq in range(2)]
                    U_b = cmatb[:, 0, :]
                    SL_b = cmatb[:, 1, :]
                    Rk = [[Rb.k(w * 1024 + hq * 512, w * 1024 + (hq + 1) * 512) for hq in range(2)] for w in range(2)]
                    rbuild(g, c)
                    bx, kxs = pb()
                    for i in range(4):
                        TR(bx[:, i * 128:(i + 1) * 128], cacc.ap[:, i, cs], ID_f, cacc.k(i * NT, (i + 1) * NT) + ["cmat"], [kxs])
                    bc, kc = pb()
                    MM(bc[:, 0:128], BT.ap[:, cs], CT.ap[:, cs], True, True, BT.k() + CT.k(), [kc])
                    bss = []
                    for hq in range(2):
                        bs, ks = pb()
                        for w in range(2):
                            MM(bs[:], SL_b, Rb.ap[:, w, 4 * hq:4 * hq + 4, :].rearrange("p h n -> p (h n)"), w == 0, w == 1,
                               Rk[w][hq] + ["cmatb"], [ks])
                        bss.append((bs, ks))
                    for hq in range(2):
                        ACT(Lb_a[hq].rearrange("p h n -> p (h n)"), bss[hq][0][:], AF.Exp, [bss[hq][1]], Lb_k[hq])
                    bx3 = bx[:].rearrange("p (h n) -> p h n", h=8)
                    TT("dve", xdt_a.rearrange("p (h n) -> p h n", h=8), bx3, bc8(DT, c, g, 64), ALU.mult, [kxs] + DT.k(), xdt_k)
                    TT("dve", xdd_a.rearrange("p (h n) -> p h n", h=8), bx3, bc8(DTD, c, g, 64), ALU.mult, [kxs] + DTD.k(), xdd_k)
                    TT("dve", xsD_a.rearrange("p (h n) -> p h n", h=8), bx3,
                       rowc[:, 2, 8 * g:8 * g + 8].unsqueeze(2).broadcast_to([128, 8, 64]), ALU.mult, [kxs, "rowc"], xsD_k)
                    TT("dve", CBm_a, bc[:, 0:128], U_f, ALU.mult, [kc, "cmat"], CBm_k)

                def stageB(c, pr):
                    cs = slice(c * 128, (c + 1) * 128)
                    xdt_a, xdt_k = xdt.ap[:, pr, :], xdt.k(pr * NT, (pr + 1) * NT)
                    xdd_a, xdd_k = xdd.ap[:, pr, :], xdd.k(pr * NT, (pr + 1) * NT)
                    xsD_a, xsD_k = xsD.ap[:, pr, :], xsD.k(pr * NT, (pr + 1) * NT)
                    CBm_a, CBm_k = CBm.ap[:, pr, :], CBm.k(pr * 128, (pr + 1) * 128)
                    t1_a, t1_k = t1.ap[:, pr, :], t1.k(pr * NT, (pr + 1) * NT)
                    MT_a, MT_k = MT.ap[:, pr], MT.k(pr * 1024, (pr + 1) * 1024)
                    Lb_a = [Lb.ap[:, pr, 4 * hq:4 * hq + 4] for hq in range(2)]
                    Lb_k = [Lb.k(pr * 1024 + hq * 512, pr * 1024 + (hq + 1) * 512) for hq in range(2)]
                    for hq in range(2):
                        TT(SSD_SB_ENG, MT_a[:, 4 * hq:4 * hq + 4, :], Lb_a[hq], CBm_a.unsqueeze(1).broadcast_to([128, 4, 128]), ALU.mult,
                           Lb_k[hq] + CBm_k, MT_k)
                    bo, ko = pb()
                    MM(bo[:], CT.ap[:, cs], Sbf[:], True, True, CT.k() + ["Sbf"], [ko])
                    bst, kst = pb()
                    MM(bst[:], Btm.ap[:, c, :], xdd_a, True, True, Btm.k() + xdd_k, [kst])
                    by, ky = pb()
                    for h8 in range(8):
                        MM(by[:, h8 * 64:(h8 + 1) * 64], MT_a[:, h8, :], xdt_a[:, h8 * 64:(h8 + 1) * 64], True, True,
                           MT_k + xdt_k, [ky])
                    TT("dve", t1_a.rearrange("p (h n) -> p h n", h=8), bo[:].rearrange("p (h n) -> p h n", h=8), bc8(SD, c, g, 64),
                       ALU.mult, [ko] + SD.k(), t1_k)
                    Sg3 = S[:, g, :].rearrange("p (h n) -> p h n", h=8)
                    TT(SSD_SB_ENG, Sg3, Sg3, bc8(CDc, c, g, 64), ALU.mult, [Sk] + CDc.k(), [Sk])
                    TT("dve", t1_a, t1_a, xsD_a, ALU.add, t1_k + xsD_k, t1_k)
                    TT("dve", S[:, g, :], S[:, g, :], bst[:], ALU.add, [Sk, kst], [Sk])
                    if c < 3:
                        CP("act", Sbf[:], S[:, g, :], [Sk], ["Sbf"])
                    TT("dve", t1_a, by[:], t1_a, ALU.add, [ky] + t1_k, t1_k)
                    TT(SSD_SB_ENG, t1_a, t1_a, sz.ap[:, c, :], ALU.mult, t1_k + sz.k(c * NT, (c + 1) * NT), t1_k)
                    ACT(junk.ap, t1_a, AF.Square, t1_k, SSQ.k(), accum=SSQ.ap[:, c, g:g + 1])

                def stageT(c, pr):
                    cs = slice(c * 128, (c + 1) * 128)
                    t1_a, t1_k = t1.ap[:, pr, :], t1.k(pr * NT, (pr + 1) * NT)
                    bt, kt = pb()
                    for i in range(4):
                        TR(bt[:, i * 128:(i + 1) * 128], t1_a[:, i * 128:(i + 1) * 128], ID_f, t1_k + ["cmat"], [kt])
                    for i in range(4):
                        nw = colc[:, C_NW + 4 * g + i:C_NW + 4 * g + i + 1]
                        ACT(ygT.ap[:, 4 * g + i, cs], bt[:, i * 128:(i + 1) * 128], AF.Identity, [kt, "colc"],
                            ygT.k((4 * g + i) * NT, (4 * g + i + 1) * NT), scale=nw)

                prs = [(cnt + c) % 2 for c in range(4)]
                cnt += 4
                stageA(0, prs[0])
                stageA(1, prs[1])
                for c in range(4):
                    stageB(c, prs[c])
                    if c + 2 < 4:
                        stageA(c + 2, prs[c + 2])
                    if c == 2 and g < 7:
                        npa, npb = inproj_parts(g + 1, 0), inproj_parts(g + 1, 1)
                        npa[0]()
                        npb[0]()
                        carry[(g + 1, 0)] = (npa[1:], npb[1:])
                    if c == 3 and g < 7:
                        carry[(g + 1, 2)] = (inproj_parts(g + 1, 2), inproj_parts(g + 1, 3))
                    stageT(c, prs[c])

            P.op("dve", lambda h: h.reduce_sum(out=RS.ap[:, 0:4], in_=SSQ.ap, axis=AX.X), SSQ.k(), RS.k())
            ACT(RS.ap[:, 0:4], RS.ap[:, 0:4], AF.Ln, RS.k(), RS.k(), bias=RMS_EPS, scale=1.0 / DIN)
            ACT(RS.ap[:, 0:4], RS.ap[:, 0:4], AF.Exp, RS.k(), RS.k(), scale=-0.5)
            br, kr = pb()
            for c in range(4):
                P.op("dve", (lambda c: lambda h: h.tensor_scalar_mul(out=Dg.ap, in0=ID_f, scalar1=RS.ap[:, c:c + 1]))(c),
                     RS.k() + ["cmat"], Dg.k())
                MM(br[:, c * 128:(c + 1) * 128], ONES_f, Dg.ap, True, True, Dg.k() + ["cmat"], [kr])
            CP("act", RBc.ap, br[:], [kr], RBc.k())
            for mg in range(4):
                banks = [pb() for _ in range(4)]
                for ib in range(8):
                    unit, ukey = loadB(("ow", mg, ib), ow_src(mg, ib))
                    for il in range(4):
                        i = 4 * ib + il
                        for ml in range(4):
                            MM(banks[ml][0][:], unit[:, il, ml * 128:(ml + 1) * 128], ygT.ap[:, i, :], i == 0, i == 31,
                               [ukey] + ygT.k(i * NT, (i + 1) * NT), [banks[ml][1]])
                for ml in range(4):
                    m = 4 * mg + ml
                    xm = xres[:, m, HALO:]
                    tb = tb2.ap[:, ml % 2, :]
                    tbk = tb2.k((ml % 2) * NT, (ml % 2 + 1) * NT)
                    TT("dve", tb, banks[ml][0][:], RBc.ap, ALU.mult, [banks[ml][1]] + RBc.k(), tbk)
                    STT("dve", xm, tb, INVA, xm, ALU.mult, ALU.add, tbk + [xk(m)], [xk(m)])

        nextp = {}
        for q in range(n_pass):
            t0 = q * NT
            if q == 0:
                for i in range(4):
                    dst = xres[:, 4 * i:4 * i + 4, HALO:]
                    src = xT_v[:, 4 * i:4 * i + 4, 0:NT]
                    P.dma("sp", (lambda dst, src: lambda h: h.dma_start(out=dst, in_=src))(dst, src), chx[i], writes=XK[4 * i:4 * i + 4])

            def st_pool():
                if q == 0:
                    pool_prefetch()
                    pooled, th = pool_front(0, xres[:], lambda a, b: XK[a:b], True)
                    for f in th:
                        f()
                    pool_back(pooled, xres[:], lambda a, b: XK[a:b])
                else:
                    xin = nextp["xin"]
                    pool_back(nextp["pooled"], xin.ap, lambda a, b: xin.k(a * XW, b * XW))
                mlp_prefetch(0)

            def st_mlp0():
                mlp(0)
                ple_prefetch(0)

            def st_ple0():
                ple(0, t0)
                if do_l1 and dbg != "ple0":
                    ssd_prefetch(0)

            def st_ssd():
                ssd_mixer(q)
                mlp_prefetch(1)

            def st_mlp1():
                mlp(1)
                ple_prefetch(1)

            def st_ple1():
                if q + 1 < n_pass and dbg is None:
                    def start():
                        xin = aalloc([16, XW])
                        t1n = (q + 1) * NT
                        for i in range(4):
                            dst = xin.ap[:, 4 * i:4 * i + 4, :]
                            src = xT_v[:, 4 * i:4 * i + 4, t1n - HALO:t1n + NT]
                            P.dma("sp", (lambda dst, src: lambda h: h.dma_start(out=dst, in_=src))(dst, src), chx[i],
                                  writes=xin.k(4 * i * XW, (4 * i + 4) * XW))
                        pooled, th = pool_front(q + 1, xin.ap, lambda a, b: xin.k(a * XW, b * XW), False)
                        nextp["xin"] = xin
                        nextp["pooled"] = pooled
                        return th
                    ple(1, t0, extra=dict(start=start, th=None))
                    pool_prefetch()
                else:
                    ple(1, t0)
            stages = [("pool", st_pool), ("ln0", lambda: layer_norm(0)), ("mlp0", st_mlp0),
                      ("ln1", lambda: layer_norm(1)), ("ple0", st_ple0)]
            if do_l1:
                stages += [("ssd", st_ssd), ("ln2", lambda: layer_norm(2)), ("mlp1", st_mlp1),
                           ("ln3", lambda: layer_norm(3)), ("ple1", st_ple1)]
            for name, fn in stages:
                fn()
                if dbg == name:
                    break
            store_out(t0)
        P.barrier("sp", ["out%d" % i for i in range(4)])
        with nc.Block() as block:
            P.emit(block)
        stats = P.stats
    return nc, stats


def host_consts(inputs):
    f = np.float32
    colc = np.zeros((128, NCOL), f)

    def cols(v):
        return np.ascontiguousarray(np.asarray(v, f).reshape(-1, 128).T)
    for i in range(2):
        for j in range(2):
            li = i * 2 + j
            colc[:, C_LNG + li * 16:C_LNG + li * 16 + 16] = cols(inputs["ln_g"][i, j])
            colc[:, C_LNB + li * 16:C_LNB + li * 16 + 16] = cols(inputs["ln_b"][i, j])
    colc[:, C_PS:C_PS + 16] = cols(inputs["pool_scale"][0])
    for k in range(4):
        colc[:, C_CW + k * 48:C_CW + (k + 1) * 48] = cols(inputs["ssm_conv_w"][0, k])
    colc[:, C_CB:C_CB + 48] = cols(inputs["ssm_conv_b"][0])
    colc[:, C_NW:C_NW + 32] = cols(inputs["ssm_norm_w"][0])
    rowc = np.zeros((128, 3, 64), f)
    rowc[:, 0, :] = np.asarray(inputs["ssm_dt_bias"][0], f)[None, :]
    rowc[:, 1, :] = np.asarray(inputs["ssm_a_log"][0], f)[None, :]
    rowc[:, 2, :] = np.asarray(inputs["ssm_d"][0], f)[None, :]
    j = np.arange(128)
    cmat = np.zeros((128, 4, 128), f)
    cmat[:, 0, :] = (j[:, None] <= j[None, :])
    cmat[:, 1, :] = (j[:, None] > j[None, :])
    cmat[:, 2, :] = 1.0
    cmat[:, 3, :] = np.eye(128, dtype=f)
    invc = np.zeros((128, 4, 16), f)
    for g, w in enumerate(WINS):
        invc[:, g, :] = 1.0 / np.minimum(np.arange(16) + 1, w).astype(f)
    return colc, rowc.reshape(128, 192), cmat.reshape(128, 512), invc.reshape(128, 64)


def make_in_maps(inputs, cores):
    f = np.float32
    colc, rowc, cmat, invc = host_consts(inputs)
    def slabs(w, ns):
        return np.ascontiguousarray(w.reshape(16, 128, ns, 256).transpose(2, 1, 0, 3)).reshape(ns, 128, 4096)

    def units(w, nj):
        return np.ascontiguousarray(w.reshape(nj, 4, 128, 4, 512).transpose(3, 0, 2, 1, 4)).reshape(4, nj, 128, 2048)
    inw = np.asarray(inputs["ssm_in_w"], f)[0]
    inr = np.empty((8, 5, 128, 4096), f)
    for g in range(8):
        blocks = [inw[:, DIN + g * 512:DIN + g * 512 + 256], inw[:, DIN + g * 512 + 256:DIN + (g + 1) * 512],
                  np.concatenate([inw[:, 2 * DIN + g * 128:2 * DIN + (g + 1) * 128],
                                  inw[:, 2 * DIN + 1024 + g * 128:2 * DIN + 1024 + (g + 1) * 128]], axis=1),
                  inw[:, g * 512:g * 512 + 256], inw[:, g * 512 + 256:(g + 1) * 512]]
        for j, blk in enumerate(blocks):
            inr[g, j] = blk.reshape(16, 128, 256).transpose(1, 0, 2).reshape(128, 4096)
    w1 = np.asarray(inputs["mlp_w1"], f)
    w2 = np.asarray(inputs["mlp_w2"], f)
    gw = np.asarray(inputs["ple_gate_w"], f)
    shared = {
        "pool_w": slabs(np.asarray(inputs["pool_w"], f).reshape(2048, 512), 2),
        "in_w": inr,
        "wdt_in": np.ascontiguousarray(inw[:, 10240:10304].reshape(16, 128, 64).transpose(1, 0, 2)).reshape(128, 1024),
        "out_w": units(np.asarray(inputs["ssm_out_w"], f)[0], 8),
        "w1": np.stack([slabs(w1[i], 32) for i in range(2)]),
        "w2": np.stack([units(w2[i], 16) for i in range(2)]),
        "ple_w": np.ascontiguousarray(np.asarray(inputs["ple_w"], f)),
        "gate_w": np.stack([slabs(gw[i], 8) for i in range(2)]),
        "colc": colc, "rowc": rowc, "cmat": cmat, "invc": invc,
    }
    x = np.asarray(inputs["x"], f)
    p = np.asarray(inputs["p"], f)
    maps = []
    for b in cores:
        m = dict(shared)
        m["xT"] = np.ascontiguousarray(x[b].T)
        m["pT"] = np.ascontiguousarray(np.transpose(p[:, b], (0, 2, 1)))
        maps.append(m)
    return maps


_CACHE = {}


def kernel(**inputs):
    if "nc" not in _CACHE:
        _CACHE["nc"] = build_program()[0]
    nc = _CACHE["nc"]
    maps = make_in_maps(inputs, list(range(8)))
    res = run_bass_kernel_spmd(nc, maps, core_ids=list(range(8)))
    out = np.stack([np.ascontiguousarray(r["outT"].T) for r in res.results], axis=0)
    return out.astype(np.float32)
```

```python
import numpy as np
from contextlib import ExitStack
import concourse.bass as bass
import concourse.mybir as mybir
from concourse.bass_utils import run_bass_kernel_spmd

F32 = mybir.dt.float32
BF16 = mybir.dt.bfloat16
ALU = mybir.AluOpType
AF = mybir.ActivationFunctionType
AX = mybir.AxisListType

ENGS = ("pe", "act", "dve", "pool", "sp")
LN_AFF_ENG = "dve"
SSD_SB_ENG = "dve"
SELF_SYNC = {"act", "dve", "pool"}

D = 2048
SEQ = 2048
NT = 512
NPASS = SEQ // NT
HALO = 16
XW = NT + HALO
DFF = 8192
DIN = 4096
NPROJ = 10304
ALPHA = 2.0 ** 0.5
INVA = 1.0 / ALPHA
LN_EPS = 1e-5 / (ALPHA * ALPHA)
RMS_EPS = 1e-5
WINS = (2, 4, 8, 16)

C_LNG = 0
C_LNB = 64
C_PS = 128
C_CW = 144
C_CB = 336
C_NW = 384
NCOL = 416


class Chan:
    def __init__(self, sem):
        self.sem = sem
        self.total = 0


class Prog:
    def __init__(self, nc, es):
        self.nc = nc
        self.es = es
        self.ops = {e: [] for e in ENGS}
        self.lastw = {}
        self.readers = {}
        self.esem = {e: es.enter_context(nc.semaphore("s_" + e)) for e in ENGS}
        self.nchan = 0

    def chan(self):
        self.nchan += 1
        return Chan(self.es.enter_context(self.nc.semaphore("c%d" % self.nchan)))

    def _deps(self, reads, writes):
        deps = []
        for r in reads:
            w = self.lastw.get(r)
            if w is not None:
                deps.append(w)
        for w in writes:
            lw = self.lastw.get(w)
            if lw is not None:
                deps.append(lw)
            deps.extend(self.readers.get(w, {}).values())
        return deps

    def _update(self, reads, writes, tok):
        key = (tok[0], tok[1] if tok[0] == "eng" else id(tok[1]))
        for r in reads:
            self.readers.setdefault(r, {})[key] = tok
        for w in writes:
            self.lastw[w] = tok
            self.readers[w] = {}

    def op(self, eng, fn, reads=(), writes=()):
        deps = self._deps(reads, writes)
        idx = len(self.ops[eng])
        self.ops[eng].append(dict(fn=fn, deps=deps, dma=None))
        self._update(reads, writes, ("eng", eng, idx))

    def dma(self, eng, fn, ch, reads=(), writes=()):
        deps = self._deps(reads, writes)
        if ch.total > 0:
            deps.append(("dma", ch, ch.total))
        ch.total += 16
        self.ops[eng].append(dict(fn=fn, deps=deps, dma=ch))
        self._update(reads, writes, ("dma", ch, ch.total))

    def barrier(self, eng, reads):
        deps = self._deps(reads, ())
        self.ops[eng].append(dict(fn=None, deps=deps, dma=None))

    def emit(self, block):
        for e in ENGS:
            seen = {}
            for i, o in enumerate(self.ops[e]):
                need = {}
                chs = {}
                for d in o["deps"]:
                    if d[0] == "eng":
                        if d[1] == e and (e not in SELF_SYNC and o["dma"] is None):
                            continue
                        k = ("eng", d[1])
                        v = d[2]
                    else:
                        k = ("dma", id(d[1]))
                        v = d[2]
                        chs[k] = d[1]
                    if v > seen.get(k, -1) and v > need.get(k, -1):
                        need[k] = v
                o["need"] = need
                o["chs"] = chs
                for k, v in need.items():
                    seen[k] = v
        sig = {e: set() for e in ENGS}
        for e in ENGS:
            for o in self.ops[e]:
                for k, v in o["need"].items():
                    if k[0] == "eng":
                        sig[k[1]].add(v)
        sigval = {}
        for e in ENGS:
            c = 0
            for i, o in enumerate(self.ops[e]):
                o["sig"] = False
                if i in sig[e]:
                    assert o["dma"] is None and o["fn"] is not None
                    c += 1
                    sigval[(e, i)] = c
                    o["sig"] = True
        self.stats = {e: (len(self.ops[e]), len(sig[e])) for e in ENGS}

        def run(e, h):
            for o in self.ops[e]:
                for k, v in o["need"].items():
                    if k[0] == "eng":
                        h.wait_ge(self.esem[k[1]], sigval[(k[1], v)])
                    else:
                        h.wait_ge(o["chs"][k].sem, v)
                if o["fn"] is None:
                    continue
                inst = o["fn"](h)
                if o["dma"] is not None:
                    inst.then_inc(o["dma"].sem, 16)
                elif o["sig"]:
                    inst.then_inc(self.esem[e], 1)

        @block.tensor
        def _(h):
            run("pe", h)

        @block.scalar
        def _(h):
            run("act", h)

        @block.vector
        def _(h):
            run("dve", h)

        @block.gpsimd
        def _(h):
            run("pool", h)

        @block.sync
        def _(h):
            run("sp", h)


class ABuf:
    def __init__(self, ap, off, nbytes, esize):
        self.ap = ap
        self.off = off
        self.nbytes = nbytes
        self.esize = esize

    def k(self, lo=None, hi=None):
        if lo is None:
            b0, b1 = self.off, self.off + self.nbytes
        else:
            b0, b1 = self.off + lo * self.esize, self.off + hi * self.esize
        return [("ar", b) for b in range(b0 // 1024, (b1 - 1) // 1024 + 1)]


ARENA_BYTES = 97 * 1024


def build_program(n_pass=NPASS, do_l1=True, dbg=None):
    nc = bass.Bass("TRN2", target_bir_lowering=False)

    def din(name, shape):
        return nc.dram_tensor(name, shape, F32, kind="ExternalInput").ap()

    xT_d = din("xT", [D, SEQ])
    pT_d = din("pT", [2, 256, SEQ])
    poolw_d = din("pool_w", [2, 128, 4096])
    inw_d = din("in_w", [8, 5, 128, 4096])
    wdt_d = din("wdt_in", [128, 16 * 64])
    outw_d = din("out_w", [4, 8, 128, 2048])
    w1_d = din("w1", [2, 32, 128, 4096])
    w2_d = din("w2", [2, 4, 16, 128, 2048])
    plew_d = din("ple_w", [2, 256, D])
    gw_d = din("gate_w", [2, 8, 128, 4096])
    colc_d = din("colc", [128, NCOL])
    rowc_d = din("rowc", [128, 3 * 64])
    cmat_d = din("cmat", [128, 4 * 128])
    invc_d = din("invc", [128, 4 * 16])
    outT_d = nc.dram_tensor("outT", [D, SEQ], F32, kind="ExternalOutput").ap()

    es = ExitStack()
    with es:
        P = Prog(nc, es)

        def sb(name, shape, dt=F32):
            return es.enter_context(nc.sbuf_tensor(name, shape, dt))

        xres = sb("xres", [128, 16, XW])
        xbf = sb("xbf", [128, 16, NT], BF16)
        colc = sb("colc_s", [128, NCOL])
        colx = sb("colx", [128, 16])
        rowc = sb("rowc_s", [128, 3, 64])
        arow = sb("arow", [128, 64])
        cmat = sb("cmat_s", [128, 4, 128])
        cmatb = sb("cmatb", [128, 4, 128], BF16)
        invc = sb("invc_s", [128, 4, 16])
        S = sb("S", [128, 8, 512])
        Sbf = sb("Sbf", [128, 512], BF16)
        ctail = sb("ctail", [128, 48, 3])
        ringA = sb("ringA", [128, 3, 16 * 256], BF16)
        ringB = sb("ringB", [128, 3, 4 * 512], BF16)
        wdt = sb("wdt", [128, 16, 64], BF16)
        arena = sb("arena", [128, ARENA_BYTES // 2], BF16)

        psf = [es.enter_context(nc.psum_tensor("ps%d" % i, [128, 512], F32)) for i in range(7)]
        psb = es.enter_context(nc.psum_tensor("psb", [128, 1024], BF16))
        U_f = cmat[:, 0, :]
        SL_f = cmat[:, 1, :]
        ONES_f = cmat[:, 2, :]
        ID_f = cmat[:, 3, :]
        ID_b = cmatb[:, 3, :]

        state = dict(bank=0, aoff=0, ra=0, rb=0)

        def pb():
            i = state["bank"]
            state["bank"] = (i + 1) % 7
            return psf[i], "ps%d" % i

        def areset():
            state["aoff"] = 0

        def aalloc(shape, dt=F32):
            esize = 4 if dt == F32 else 2
            n = int(np.prod(shape))
            nbytes = n * esize
            off = state["aoff"]
            state["aoff"] = (off + nbytes + 63) // 64 * 64
            assert state["aoff"] <= ARENA_BYTES, ("arena overflow", state["aoff"])
            v = arena[:, off // 2:(off + nbytes) // 2]
            if dt == F32:
                v = v.bitcast(F32)
            if len(shape) == 2:
                v = v.rearrange("p (a b) -> p a b", a=shape[0])
            elif len(shape) == 3:
                v = v.rearrange("p (a b c) -> p a b c", a=shape[0], b=shape[1])
            return ABuf(v, off, nbytes, esize)

        chA = [P.chan() for _ in range(3)]
        chB = [P.chan() for _ in range(3)]
        pendA = []
        pendB = []

        def _issueA(srcs):
            i = state["ra"]
            state["ra"] = (i + 1) % 3
            slot = ringA[:, i, :].rearrange("p (k n) -> p k n", k=16)
            key = "ringA%d" % i
            flat = ringA[:, i, :]
            P.dma("pool", lambda h: h.dma_start(out=flat, in_=srcs), chA[i], writes=[key])
            return slot, key

        def prefetchA(tag, srcs):
            pendA.append((tag, _issueA(srcs)))

        def loadA(tag, srcs):
            if pendA and pendA[0][0] == tag:
                return pendA.pop(0)[1]
            assert not pendA, (tag, pendA[0][0])
            return _issueA(srcs)

        def _issueB(src):
            i = state["rb"]
            state["rb"] = (i + 1) % 3
            slot = ringB[:, i, :].rearrange("p (j n) -> p j n", j=4)
            key = "ringB%d" % i
            flat = ringB[:, i, :]
            P.dma("pool", lambda h: h.dma_start(out=flat, in_=src), chB[i], writes=[key])
            return slot, key

        def prefetchB(tag, src):
            pendB.append((tag, _issueB(src)))

        def loadB(tag, src):
            if pendB and pendB[0][0] == tag:
                return pendB.pop(0)[1]
            assert not pendB, (tag, pendB[0][0])
            return _issueB(src)

        def MM(out, lhsT, rhs, st, sp, R, W):
            P.op("pe", lambda h: h.matmul(out, lhsT=lhsT, rhs=rhs, start=st, stop=sp), R, W)

        def TR(out, in_, ident, R, W):
            P.op("pe", lambda h: h.transpose(out, in_, ident), R, W)

        def ACT(out, in_, func, R, W, bias=None, scale=None, accum=None):
            kw = {}
            if bias is not None:
                kw["bias"] = bias
            if scale is not None:
                kw["scale"] = scale
            if accum is not None:
                kw["accum_out"] = accum
            P.op("act", lambda h: h.activation(out=out, in_=in_, func=func, **kw), R, W)

        def TT(eng, out, in0, in1, op, R, W):
            P.op(eng, lambda h: h.tensor_tensor(out=out, in0=in0, in1=in1, op=op), R, W)

        def TS(eng, out, in0, s1, s2, op0, op1, R, W):
            P.op(eng, lambda h: h.tensor_scalar(out=out, in0=in0, scalar1=s1, scalar2=s2, op0=op0, op1=op1), R, W)

        def STT(eng, out, in0, sc, in1, op0, op1, R, W):
            P.op(eng, lambda h: h.scalar_tensor_tensor(out=out, in0=in0, scalar=sc, in1=in1, op0=op0, op1=op1), R, W)

        def CP(eng, out, in_, R, W):
            if eng == "act":
                P.op("act", lambda h: h.copy(out=out, in_=in_), R, W)
            else:
                P.op(eng, lambda h: h.tensor_copy(out=out, in_=in_), R, W)

        def MSET(eng, ap, val, W):
            P.op(eng, lambda h: h.memset(ap, val), (), W)

        def xk(m):
            return "xres%d" % m

        def bk(m):
            return "xbf%d" % m

        XK = [xk(m) for m in range(16)]
        BK = [bk(m) for m in range(16)]

        chc = P.chan()
        for (dst, src, key) in ((colc[:], colc_d, "colc"), (rowc[:].rearrange("p a b -> p (a b)"), rowc_d, "rowc"),
                                (cmat[:].rearrange("p a b -> p (a b)"), cmat_d, "cmat"),
                                (invc[:].rearrange("p a b -> p (a b)"), invc_d, "invc")):
            P.dma("sp", (lambda dst, src: lambda h: h.dma_start(out=dst, in_=src))(dst, src), chc, writes=[key])
        chw = P.chan()
        P.dma("pool", lambda h: h.dma_start(out=wdt[:].rearrange("p k n -> p (k n)"), in_=wdt_d), chw, writes=["wdt"])
        CP("dve", cmatb[:], cmat[:], ["cmat"], ["cmatb"])
        P.op("dve", lambda h: h.tensor_scalar_mul(out=colx[:], in0=colc[:, C_PS:C_PS + 16], scalar1=INVA), ["colc"], ["colx"])
        ACT(arow[:], rowc[:, 1, :], AF.Exp, ["rowc"], ["arow"])
        P.op("dve", lambda h: h.tensor_scalar_mul(out=arow[:], in0=arow[:], scalar1=-1.0), ["arow"], ["arow"])
        MSET("dve", S[:], 0.0, ["S%d" % g for g in range(8)])
        MSET("dve", ctail[:], 0.0, ["ctail%d" % i for i in range(48)])
        MSET("dve", xres[:, :, 0:HALO], 0.0, XK)

        chx = [P.chan() for _ in range(4)]
        cho = [P.chan() for _ in range(4)]
        chp = P.chan()
        chpw = P.chan()
        xT_v = xT_d.rearrange("(c p) t -> p c t", p=128)
        outT_v = outT_d.rearrange("(c p) t -> p c t", p=128)

        def store_out(t0):
            for i in range(4):
                src = xres[:, 4 * i:4 * i + 4, HALO:]
                dst = outT_v[:, 4 * i:4 * i + 4, t0:t0 + NT]
                P.dma("sp", (lambda dst, src: lambda h: h.dma_start(out=dst, in_=src))(dst, src), cho[i],
                      reads=XK[4 * i:4 * i + 4], writes=["out%d" % i])

        ONES_b = cmatb[:, 2, :]

        def layer_norm(li):
            areset()
            zb = aalloc([2, NT], BF16)
            sq = aalloc([2, NT], BF16)
            mean = aalloc([NT])
            msq = aalloc([NT])
            rstd = aalloc([NT])
            nmr = aalloc([NT])
            tt = aalloc([16, NT])
            b1, k1 = pb()
            b2, k2 = pb()
            for m in range(16):
                xm = xres[:, m, HALO:]
                zk = zb.k((m % 2) * NT, (m % 2 + 1) * NT)
                sk = sq.k((m % 2) * NT, (m % 2 + 1) * NT)
                CP("dve", zb.ap[:, m % 2, :], xm, [xk(m)], zk)
                ACT(sq.ap[:, m % 2, :], xm, AF.Square, [xk(m)], sk)
                MM(b1[:], ONES_b, zb.ap[:, m % 2, :], m == 0, m == 15, zk + ["cmatb"], [k1])
                MM(b2[:], ONES_b, sq.ap[:, m % 2, :], m == 0, m == 15, sk + ["cmatb"], [k2])
            P.op("act", lambda h: h.mul(out=mean.ap, in_=b1[:], mul=1.0 / D), [k1], mean.k())
            TT("dve", msq.ap, mean.ap, mean.ap, ALU.mult, mean.k(), msq.k())
            STT("dve", rstd.ap, b2[:], 1.0 / D, msq.ap, ALU.mult, ALU.subtract, [k2] + msq.k(), rstd.k())
            ACT(rstd.ap, rstd.ap, AF.Ln, rstd.k(), rstd.k(), bias=LN_EPS, scale=1.0)
            ACT(rstd.ap, rstd.ap, AF.Exp, rstd.k(), rstd.k(), scale=-0.5)
            def ln_sub(p):
                m0 = 2 * p
                t2k = tt.k(m0 * NT, (m0 + 2) * NT)
                TT("dve", tt.ap[:, m0:m0 + 2, :], xres[:, m0:m0 + 2, HALO:], mean.ap.unsqueeze(1).broadcast_to([128, 2, NT]), ALU.subtract,
                   [xk(m0), xk(m0 + 1)] + mean.k(), t2k)

            def ln_mul(p):
                m0 = 2 * p
                t2 = tt.ap[:, m0:m0 + 2, :]
                t2k = tt.k(m0 * NT, (m0 + 2) * NT)
                TT("dve", t2, t2, rstd.ap.unsqueeze(1).broadcast_to([128, 2, NT]), ALU.mult, t2k + rstd.k(), t2k)
                for m in (m0, m0 + 1):
                    g_ap = colc[:, C_LNG + li * 16 + m:C_LNG + li * 16 + m + 1]
                    b_ap = colc[:, C_LNB + li * 16 + m:C_LNB + li * 16 + m + 1]
                    ACT(xbf[:, m, :], tt.ap[:, m, :], AF.Identity, tt.k(m * NT, (m + 1) * NT) + ["colc"], [bk(m)], bias=b_ap, scale=g_ap)

            ln_sub(0)
            ln_sub(1)
            for p in range(8):
                ln_mul(p)
                if p + 2 < 8:
                    ln_sub(p + 2)
            for m in range(16):
                g_ap = colc[:, C_LNG + li * 16 + m:C_LNG + li * 16 + m + 1]
                b_ap = colc[:, C_LNB + li * 16 + m:C_LNB + li * 16 + m + 1]
                TS(LN_AFF_ENG, xres[:, m, HALO:], tt.ap[:, m, :], g_ap, b_ap, ALU.mult, ALU.add, tt.k(m * NT, (m + 1) * NT) + ["colc"], [xk(m)])

        def w1_src(i, s):
            return w1_d[i, s]

        def mlp_prefetch(i):
            for s in range(3):
                prefetchA(("w1", i, s), w1_src(i, s))

        def mlp(i):
            areset()
            hT = aalloc([64, NT], BF16)
            rtmp = aalloc([2, NT])
            def evac1(j, b, kb):
                r = rtmp.ap[:, j % 2, :]
                rk = rtmp.k((j % 2) * NT, (j % 2 + 1) * NT)
                ACT(r, b[:], AF.Relu, [kb], rk)
                TT("dve", hT.ap[:, j, :], r, r, ALU.mult, rk, hT.k(j * NT, (j + 1) * NT))

            slabs01 = [loadA(("w1", i, s), w1_src(i, s)) for s in range(2)]
            banks01 = [pb() for _ in range(4)]
            for k in range(16):
                for jj in range(4):
                    sl3, skey = slabs01[jj // 2]
                    MM(banks01[jj][0][:], sl3[:, k, (jj % 2) * 128:(jj % 2 + 1) * 128], xbf[:, k, :], k == 0, k == 15,
                       [skey, bk(k)], [banks01[jj][1]])
            for jj in range(4):
                evac1(jj, banks01[jj][0], banks01[jj][1])
            for s in range(2, 32):
                sl3, skey = loadA(("w1", i, s), w1_src(i, s))
                for jl in range(2):
                    j = 2 * s + jl
                    b, kb = pb()
                    for k in range(16):
                        MM(b[:], sl3[:, k, jl * 128:(jl + 1) * 128], xbf[:, k, :], k == 0, k == 15, [skey, bk(k)], [kb])
                    evac1(j, b, kb)
            for mg in range(4):
                banks = [pb() for _ in range(4)]
                for jb in range(16):
                    unit, ukey = loadB(("w2", i, mg, jb), w2_d[i, mg, jb])
                    for jl in range(4):
                        j = 4 * jb + jl
                        for ml in range(4):
                            MM(banks[ml][0][:], unit[:, jl, ml * 128:(ml + 1) * 128], hT.ap[:, j, :], j == 0, j == 63,
                               [ukey] + hT.k(j * NT, (j + 1) * NT), [banks[ml][1]])
                for ml in range(4):
                    m = 4 * mg + ml
                    xm = xres[:, m, HALO:]
                    STT("dve", xm, banks[ml][0][:], INVA, xm, ALU.mult, ALU.add, [banks[ml][1], xk(m)], [xk(m)])

        def gw_src(i, s):
            return gw_d[i, s]

        def ple_prefetch(i):
            for s in range(3):
                prefetchA(("gw", i, s), gw_src(i, s))

        def ple(i, t0, extra=None):
            areset()
            plw = aalloc([2, D], BF16)
            sig = aalloc([2, NT])
            tmp = aalloc([2, NT])
            pTs = aalloc([2, NT], BF16)
            P.dma("pool", lambda h: h.dma_start(out=pTs.ap, in_=pT_d[i].rearrange("(k p) t -> p k t", p=128)[:, :, t0:t0 + NT]), chp, writes=pTs.k())
            P.dma("pool", lambda h: h.dma_start(out=plw.ap, in_=plew_d[i].rearrange("(k p) n -> p k n", p=128)), chpw, writes=plw.k())
            def tail(m, bg, kg):
                bp, kp = pb()
                for k in range(2):
                    MM(bp[:], plw.ap[:, k, m * 128:(m + 1) * 128], pTs.ap[:, k, :], k == 0, k == 1, plw.k() + pTs.k(), [kp])
                sg = sig.ap[:, m % 2, :]
                sk = sig.k((m % 2) * NT, (m % 2 + 1) * NT)
                tp = tmp.ap[:, m % 2, :]
                tk = tmp.k((m % 2) * NT, (m % 2 + 1) * NT)
                ACT(sg, bg[:], AF.Sigmoid, [kg], sk)
                TT("dve", tp, sg, bp[:], ALU.mult, sk + [kp], tk)
                xm = xres[:, m, HALO:]
                TT("dve", xm, xm, tp, ALU.add, [xk(m)] + tk, [xk(m)])
                if extra is not None and m == 0:
                    extra["th"] = extra["start"]()
                if extra is not None and extra["th"]:
                    extra["th"].pop(0)()

            slabs01 = [loadA(("gw", i, s), gw_src(i, s)) for s in range(2)]
            banks01 = [pb() for _ in range(4)]
            for k in range(16):
                for mm in range(4):
                    sl3, skey = slabs01[mm // 2]
                    MM(banks01[mm][0][:], sl3[:, k, (mm % 2) * 128:(mm % 2 + 1) * 128], xbf[:, k, :], k == 0, k == 15,
                       [skey, bk(k)], [banks01[mm][1]])
            for mm in range(4):
                tail(mm, banks01[mm][0], banks01[mm][1])
            for s in range(2, 8):
                sl3, skey = loadA(("gw", i, s), gw_src(i, s))
                for ml in range(2):
                    m = 2 * s + ml
                    bg, kg = pb()
                    for k in range(16):
                        MM(bg[:], sl3[:, k, ml * 128:(ml + 1) * 128], xbf[:, k, :], k == 0, k == 15, [skey, bk(k)], [kg])
                    tail(m, bg, kg)
            if extra is not None:
                while extra["th"]:
                    extra["th"].pop(0)()

        def pw_src(hf):
            return poolw_d[hf]

        def pool_prefetch():
            for hf in range(2):
                prefetchA(("pw", hf), pw_src(hf))

        def pool_front(q, X16, Xkeys, fresh):
            if fresh:
                areset()
            engs = ("dve", "pool") if q == 0 else ("dve",)
            sbufs = {e: (aalloc([4, XW]), aalloc([4, XW])) for e in engs}
            pooled = aalloc([16, NT], BF16)
            fix = {e: aalloc([4, 16]) for e in engs} if q == 0 else None
            th = []
            for g in (3, 2, 1, 0):
                eng = "dve" if (g in (3, 0) or q > 0) else "pool"
                sA, sB = sbufs[eng]
                X = X16[:, 4 * g:4 * g + 4, :]
                Xk = Xkeys(4 * g, 4 * g + 4)
                win = WINS[g]
                cur, curk = X, Xk
                sh = 1
                dst = sA
                while sh < win:
                    lo = 2 * sh - 1
                    th.append((lambda eng, dst, cur, curk, lo, sh: lambda: TT(eng, dst.ap[:, :, lo:], cur[:, :, lo:], cur[:, :, lo - sh:XW - sh],
                                                                             ALU.add, curk, dst.k()))(eng, dst, cur, curk, lo, sh))
                    cur, curk = dst.ap, dst.k()
                    dst = sB if dst is sA else sA
                    sh *= 2
                pk = pooled.k(4 * g * NT, (4 * g + 4) * NT)
                th.append((lambda g, cur, curk, X, Xk, win, pk: lambda: STT("dve", pooled.ap[:, 4 * g:4 * g + 4, :], cur[:, :, HALO:], 1.0 / win,
                                                                           X[:, :, HALO:], ALU.mult, ALU.subtract, curk + Xk, pk))(g, cur, curk, X, Xk, win, pk))
                if q == 0:
                    fx = fix[eng]
                    th.append((lambda g, fx, cur, curk: lambda: TT("dve", fx.ap, cur[:, :, HALO:HALO + 16],
                                                                  invc[:, g:g + 1, :].broadcast_to([128, 4, 16]), ALU.mult,
                                                                  curk + ["invc"], fx.k()))(g, fx, cur, curk))
                    th.append((lambda g, fx, X, Xk, pk: lambda: TT("dve", pooled.ap[:, 4 * g:4 * g + 4, 0:16], fx.ap, X[:, :, HALO:HALO + 16],
                                                                  ALU.subtract, fx.k() + Xk, pk))(g, fx, X, Xk, pk))
            return pooled, th

        def pool_back(pooled, X16, Xkeys):
            slabs = [loadA(("pw", hf), pw_src(hf)) for hf in range(2)]
            for m in range(16):
                g = m // 4
                ml = m % 4
                sl3, skey = slabs[ml // 2]
                b, kb = pb()
                for k in range(4):
                    MM(b[:], sl3[:, 4 * g + k, (ml % 2) * 128:(ml % 2 + 1) * 128], pooled.ap[:, 4 * g + k, :], k == 0, k == 3,
                       [skey] + pooled.k((4 * g + k) * NT, (4 * g + k + 1) * NT), [kb])
                xm = xres[:, m, HALO:]
                src_ok = Xkeys(m, m + 1)
                STT("dve", xm, b[:], colx[:, m:m + 1], X16[:, m, HALO:], ALU.mult, ALU.add, [kb, "colx"] + src_ok, [xk(m)])

        def ssd_srcs(g):
            return [(("inx", g, 0), inw_d[g, 0]), (("inx", g, 1), inw_d[g, 1]), (("inbc", g), inw_d[g, 2]),
                    (("inz", g, 0), inw_d[g, 3]), (("inz", g, 1), inw_d[g, 4])]

        def ow_src(mg, ib):
            return outw_d[mg, ib]

        def ssd_prefetch(g):
            for tag, srcs in ssd_srcs(g)[:3]:
                prefetchA(tag, srcs)

        def ssd_mixer(q):
            areset()
            ygT = aalloc([32, NT], BF16)
            sz = aalloc([4, NT])
            cacc = aalloc([4, NT])
            accr = aalloc([2, NT])
            ubuf = aalloc([2, 528])
            BT = aalloc([NT], BF16)
            CT = aalloc([NT], BF16)
            Btm = aalloc([4, 128], BF16)
            Rb = aalloc([2, 8, 128], BF16)
            Lb = aalloc([2, 8, 128])
            MT = aalloc([2, 8, 128], BF16)
            xdt = aalloc([2, NT], BF16)
            xdd = aalloc([2, NT], BF16)
            xsD = aalloc([2, NT])
            CBm = aalloc([2, 128])
            t1 = aalloc([2, NT])
            DT = aalloc([4, 64]); DA = aalloc([4, 64]); SD = aalloc([4, 64]); DTD = aalloc([4, 64]); CDc = aalloc([4, 64])
            SSQ = aalloc([4, 8])
            scratch0 = state["aoff"]
            V = aalloc([4, 64]); Mx = aalloc([4, 64]); Tn = aalloc([4, 64])
            ACS = Mx
            TOT = Tn
            DEC = V
            state["aoff"] = scratch0
            junk = aalloc([NT], BF16)
            scratch1 = state["aoff"]
            DAH = aalloc([4, 64]); DAL = aalloc([4, 64]); DAHb = aalloc([4, 64], BF16)
            state["aoff"] = scratch1
            RS = aalloc([16])
            Dg = aalloc([128])
            RBc = aalloc([NT])
            tb2 = t1

            for m in range(16):
                CP("act" if m % 2 else "dve", xbf[:, m, :], xres[:, m, HALO:], [xk(m)], [bk(m)])
            P.op("dve", lambda h: h.memset(SSQ.ap, 0.0), (), SSQ.k())

            for c in range(4):
                b, kb = pb()
                for k in range(16):
                    MM(b[:, 0:64], xbf[:, k, c * 128:(c + 1) * 128], wdt[:, k, :], k == 0, k == 15, [bk(k), "wdt"], [kb])
                TT("dve", V.ap[:, c, :], b[:, 0:64], rowc[:, 0, :], ALU.add, [kb, "rowc"], V.k())
            P.op("dve", lambda h: h.tensor_scalar_max(out=Mx.ap, in0=V.ap, scalar1=0.0), V.k(), Mx.k())
            STT("dve", Tn.ap, Mx.ap, -2.0, V.ap, ALU.mult, ALU.add, Mx.k() + V.k(), Tn.k())
            ACT(Tn.ap, Tn.ap, AF.Exp, Tn.k(), Tn.k())
            ACT(Tn.ap, Tn.ap, AF.Ln, Tn.k(), Tn.k(), bias=1.0, scale=1.0)
            TT("dve", DT.ap, Mx.ap, Tn.ap, ALU.add, Mx.k() + Tn.k(), DT.k())
            TT("dve", DA.ap, DT.ap, arow[:].unsqueeze(1).broadcast_to([128, 4, 64]), ALU.mult, DT.k() + ["arow"], DA.k())
            def dt_part2():
                for c in range(4):
                    b, kb = pb()
                    MM(b[:, 0:64], U_f, DA.ap[:, c, :], True, True, DA.k() + ["cmat"], [kb])
                    MM(b[:, 64:128], ONES_f, DA.ap[:, c, :], True, True, DA.k() + ["cmat"], [kb])
                    CP("act", ACS.ap[:, c, :], b[:, 0:64], [kb], ACS.k())
                    CP("act", TOT.ap[:, c, :], b[:, 64:128], [kb], TOT.k())
                ACT(SD.ap, ACS.ap, AF.Exp, ACS.k(), SD.k())
                TT("dve", DEC.ap, TOT.ap, ACS.ap, ALU.subtract, TOT.k() + ACS.k(), DEC.k())
                ACT(DEC.ap, DEC.ap, AF.Exp, DEC.k(), DEC.k())
                TT("dve", DTD.ap, DT.ap, DEC.ap, ALU.mult, DT.k() + DEC.k(), DTD.k())
                ACT(CDc.ap, TOT.ap, AF.Exp, TOT.k(), CDc.k())
                CP("dve", DAHb.ap, DA.ap, DA.k(), DAHb.k())
                CP("dve", DAH.ap, DAHb.ap, DAHb.k(), DAH.k())
                TT("dve", DAL.ap, DA.ap, DAH.ap, ALU.subtract, DA.k() + DAH.k(), DAL.k())


            def bc8(buf, c, g, n):
                return buf.ap[:, c, 8 * g:8 * g + 8].unsqueeze(2).broadcast_to([128, 8, n])

            def rbuild(g, c):
                U_b = cmatb[:, 0, :]
                for hq in range(2):
                    for w, src in ((0, DAH), (1, DAL)):
                        TT("dve", Rb.ap[:, w, 4 * hq:4 * hq + 4, :], U_b.unsqueeze(1).broadcast_to([128, 4, 128]),
                           src.ap[:, c, 8 * g + 4 * hq:8 * g + 4 * hq + 4].unsqueeze(2).broadcast_to([128, 4, 128]), ALU.mult,
                           ["cmatb"] + src.k(), Rb.k(w * 1024 + hq * 512, w * 1024 + (hq + 1) * 512))

            slabsG = {}

            def getG(gg, idx):
                if (gg, idx) not in slabsG:
                    sr = ssd_srcs(gg)
                    slabsG[(gg, idx)] = loadA(sr[idx][0], sr[idx][1])
                return slabsG[(gg, idx)]

            def inproj_parts(gg, i):
                b, kb = pb()
                if i < 4:
                    sl3, skey = getG(gg, i // 2)
                    c0 = (i % 2) * 128
                else:
                    sl3, skey = getG(gg, 2)
                    c0 = (i - 4) * 128
                for k in range(16):
                    MM(b[:], sl3[:, k, c0:c0 + 128], xbf[:, k, :], k == 0, k == 15, [skey, bk(k)], [kb])
                cc = (gg * 4 + i) if i < 4 else (32 + gg if i == 4 else 40 + gg)
                ub = ubuf.ap[:, i % 2, :]
                ubk = ubuf.k((i % 2) * 528, (i % 2 + 1) * 528)
                tk = "ctail%d" % (gg * 6 + i)
                if i < 4:
                    acc = cacc.ap[:, i, :]
                    ak = cacc.k(i * NT, (i + 1) * NT)
                else:
                    acc = accr.ap[:, i % 2, :]
                    ak = accr.k((i % 2) * NT, (i % 2 + 1) * NT)

                def wcol(kk):
                    return colc[:, C_CW + kk * 48 + cc:C_CW + kk * 48 + cc + 1]

                def fin():
                    if i < 4:
                        ACT(acc, acc, AF.Silu, ak, ak)
                    elif i == 4:
                        ACT(BT.ap, acc, AF.Silu, ak, BT.k())
                    else:
                        ACT(CT.ap, acc, AF.Silu, ak, CT.k())
                return [
                    lambda: CP("act", ub[:, 3:3 + NT], b[:], [kb], ubk),
                    lambda: CP("dve", ub[:, 0:3], ctail[:, gg * 6 + i, :], [tk], ubk),
                    lambda: CP("dve", ctail[:, gg * 6 + i, :], ub[:, NT:NT + 3], ubk, [tk]),
                    lambda: TS("dve", acc, ub[:, 0:NT], wcol(0), colc[:, C_CB + cc:C_CB + cc + 1], ALU.mult, ALU.add, ubk + ["colc"], ak),
                    lambda: STT("dve", acc, ub[:, 1:1 + NT], wcol(1), acc, ALU.mult, ALU.add, ubk + ak + ["colc"], ak),
                    lambda: STT("dve", acc, ub[:, 2:2 + NT], wcol(2), acc, ALU.mult, ALU.add, ubk + ak + ["colc"], ak),
                    lambda: STT("dve", acc, ub[:, 3:3 + NT], wcol(3), acc, ALU.mult, ALU.add, ubk + ak + ["colc"], ak),
                    fin,
                ]

            carry = {}
            cnt = 0
            for g in range(8):
                Sk = "S%d" % g
                CP("act", Sbf[:], S[:, g, :], [Sk], ["Sbf"])

                def get(idx, g=g):
                    return getG(g, idx)
                for i0 in (0, 2, 4):
                    if (g, i0) in carry:
                        pa, pb_ = carry.pop((g, i0))
                    else:
                        pa, pb_ = inproj_parts(g, i0), inproj_parts(g, i0 + 1)
                    for fa, fb in zip(pa, pb_):
                        fa()
                        fb()
                zb = [pb() for _ in range(4)]
                for hf in range(2):
                    sl3, skey = get(3 + hf)
                    for c in range(4):
                        for k in range(16):
                            MM(zb[c][0][:, hf * 256:(hf + 1) * 256], xbf[:, k, c * 128:(c + 1) * 128], sl3[:, k, :], k == 0, k == 15,
                               [skey, bk(k)], [zb[c][1]])
                for c in range(4):
                    ACT(sz.ap[:, c, :], zb[c][0][:], AF.Silu, [zb[c][1]], sz.k(c * NT, (c + 1) * NT))
                if g == 0:
                    dt_part2()
                if g < 7:
                    ssd_prefetch(g + 1)
                else:
                    for ib in range(3):
                        prefetchB(("ow", 0, ib), ow_src(0, ib))
                for c in range(4):
                    TR(psb[:, c * 128:(c + 1) * 128], BT.ap[:, c * 128:(c + 1) * 128], ID_b, BT.k() + ["cmatb"], ["psb"])
                CP("act", Btm.ap, psb[:, 0:512].rearrange("p (c n) -> p c n", c=4), ["psb"], Btm.k())

                def stageA(c, pr):
                    cs = slice(c * 128, (c + 1) * 128)
                    xdt_a, xdt_k = xdt.ap[:, pr, :], xdt.k(pr * NT, (pr + 1) * NT)
                    xdd_a, xdd_k = xdd.ap[:, pr, :], xdd.k(pr * NT, (pr + 1) * NT)
                    xsD_a, xsD_k = xsD.ap[:, pr, :], xsD.k(pr * NT, (pr + 1) * NT)
                    CBm_a, CBm_k = CBm.ap[:, pr, :], CBm.k(pr * 128, (pr + 1) * 128)
                    Lb_a = [Lb.ap[:, pr, 4 * hq:4 * hq + 4] for hq in range(2)]
                    Lb_k = [Lb.k(pr * 1024 + hq * 512, pr * 1024 + (hq + 1) * 512) for hq in range(2)]
                    U_b = cmatb[:, 0, :]
                    SL_b = cmatb[:, 1, :]
                    Rk = [[Rb.k(w * 1024 + hq * 512, w * 1024 + (hq + 1) * 512) for hq in range(2)] for w in range(2)]
                    rbuild(g, c)
                    bx, kxs = pb()
                    for i in range(4):
                        TR(bx[:, i * 128:(i + 1) * 128], cacc.ap[:, i, cs], ID_f, cacc.k(i * NT, (i + 1) * NT) + ["cmat"], [kxs])
                    bc, kc = pb()
                    MM(bc[:, 0:128], BT.ap[:, cs], CT.ap[:, cs], True, True, BT.k() + CT.k(), [kc])
                    bss = []
                    for hq in range(2):
                        bs, ks = pb()
                        for w in range(2):
                            MM(bs[:], SL_b, Rb.ap[:, w, 4 * hq:4 * hq + 4, :].rearrange("p h n -> p (h n)"), w == 0, w == 1,
                               Rk[w][hq] + ["cmatb"], [ks])
                        bss.append((bs, ks))
                    for hq in range(2):
                        ACT(Lb_a[hq].rearrange("p h n -> p (h n)"), bss[hq][0][:], AF.Exp, [bss[hq][1]], Lb_k[hq])
                    bx3 = bx[:].rearrange("p (h n) -> p h n", h=8)
                    TT("dve", xdt_a.rearrange("p (h n) -> p h n", h=8), bx3, bc8(DT, c, g, 64), ALU.mult, [kxs] + DT.k(), xdt_k)
                    TT("dve", xdd_a.rearrange("p (h n) -> p h n", h=8), bx3, bc8(DTD, c, g, 64), ALU.mult, [kxs] + DTD.k(), xdd_k)
                    TT("dve", xsD_a.rearrange("p (h n) -> p h n", h=8), bx3,
                       rowc[:, 2, 8 * g:8 * g + 8].unsqueeze(2).broadcast_to([128, 8, 64]), ALU.mult, [kxs, "rowc"], xsD_k)
                    TT("dve", CBm_a, bc[:, 0:128], U_f, ALU.mult, [kc, "cmat"], CBm_k)

                def stageB(c, pr):
                    cs = slice(c * 128, (c + 1) * 128)
                    xdt_a, xdt_k = xdt.ap[:, pr, :], xdt.k(pr * NT, (pr + 1) * NT)
                    xdd_a, xdd_k = xdd.ap[:, pr, :], xdd.k(pr * NT, (pr + 1) * NT)
                    xsD_a, xsD_k = xsD.ap[:, pr, :], xsD.k(pr * NT, (pr + 1) * NT)
                    CBm_a, CBm_k = CBm.ap[:, pr, :], CBm.k(pr * 128, (pr + 1) * 128)
                    t1_a, t1_k = t1.ap[:, pr, :], t1.k(pr * NT, (pr + 1) * NT)
                    MT_a, MT_k = MT.ap[:, pr], MT.k(pr * 1024, (pr + 1) * 1024)
                    Lb_a = [Lb.ap[:, pr, 4 * hq:4 * hq + 4] for hq in range(2)]
                    Lb_k = [Lb.k(pr * 1024 + hq * 512, pr * 1024 + (hq + 1) * 512) for hq in range(2)]
                    for hq in range(2):
                        TT(SSD_SB_ENG, MT_a[:, 4 * hq:4 * hq + 4, :], Lb_a[hq], CBm_a.unsqueeze(1).broadcast_to([128, 4, 128]), ALU.mult,
                           Lb_k[hq] + CBm_k, MT_k)
                    bo, ko = pb()
                    MM(bo[:], CT.ap[:, cs], Sbf[:], True, True, CT.k() + ["Sbf"], [ko])
                    bst, kst = pb()
                    MM(bst[:], Btm.ap[:, c, :], xdd_a, True, True, Btm.k() + xdd_k, [kst])
                    by, ky = pb()
                    for h8 in range(8):
                        MM(by[:, h8 * 64:(h8 + 1) * 64], MT_a[:, h8, :], xdt_a[:, h8 * 64:(h8 + 1) * 64], True, True,
                           MT_k + xdt_k, [ky])
                    TT("dve", t1_a.rearrange("p (h n) -> p h n", h=8), bo[:].rearrange("p (h n) -> p h n", h=8), bc8(SD, c, g, 64),
                       ALU.mult, [ko] + SD.k(), t1_k)
                    Sg3 = S[:, g, :].rearrange("p (h n) -> p h n", h=8)
                    TT(SSD_SB_ENG, Sg3, Sg3, bc8(CDc, c, g, 64), ALU.mult, [Sk] + CDc.k(), [Sk])
                    TT("dve", t1_a, t1_a, xsD_a, ALU.add, t1_k + xsD_k, t1_k)
                    TT("dve", S[:, g, :], S[:, g, :], bst[:], ALU.add, [Sk, kst], [Sk])
                    if c < 3:
                        CP("act", Sbf[:], S[:, g, :], [Sk], ["Sbf"])
                    TT("dve", t1_a, by[:], t1_a, ALU.add, [ky] + t1_k, t1_k)
                    TT(SSD_SB_ENG, t1_a, t1_a, sz.ap[:, c, :], ALU.mult, t1_k + sz.k(c * NT, (c + 1) * NT), t1_k)
                    ACT(junk.ap, t1_a, AF.Square, t1_k, SSQ.k(), accum=SSQ.ap[:, c, g:g + 1])

                def stageT(c, pr):
                    cs = slice(c * 128, (c + 1) * 128)
                    t1_a, t1_k = t1.ap[:, pr, :], t1.k(pr * NT, (pr + 1) * NT)
                    bt, kt = pb()
                    for i in range(4):
                        TR(bt[:, i * 128:(i + 1) * 128], t1_a[:, i * 128:(i + 1) * 128], ID_f, t1_k + ["cmat"], [kt])
                    for i in range(4):
                        nw = colc[:, C_NW + 4 * g + i:C_NW + 4 * g + i + 1]
                        ACT(ygT.ap[:, 4 * g + i, cs], bt[:, i * 128:(i + 1) * 128], AF.Identity, [kt, "colc"],
                            ygT.k((4 * g + i) * NT, (4 * g + i + 1) * NT), scale=nw)

                prs = [(cnt + c) % 2 for c in range(4)]
                cnt += 4
                stageA(0, prs[0])
                stageA(1, prs[1])
                for c in range(4):
                    stageB(c, prs[c])
                    if c + 2 < 4:
                        stageA(c + 2, prs[c + 2])
                    if c == 2 and g < 7:
                        npa, npb = inproj_parts(g + 1, 0), inproj_parts(g + 1, 1)
                        npa[0]()
                        npb[0]()
                        carry[(g + 1, 0)] = (npa[1:], npb[1:])
                    if c == 3 and g < 7:
                        carry[(g + 1, 2)] = (inproj_parts(g + 1, 2), inproj_parts(g + 1, 3))
                    stageT(c, prs[c])

            P.op("dve", lambda h: h.reduce_sum(out=RS.ap[:, 0:4], in_=SSQ.ap, axis=AX.X), SSQ.k(), RS.k())
            ACT(RS.ap[:, 0:4], RS.ap[:, 0:4], AF.Ln, RS.k(), RS.k(), bias=RMS_EPS, scale=1.0 / DIN)
            ACT(RS.ap[:, 0:4], RS.ap[:, 0:4], AF.Exp, RS.k(), RS.k(), scale=-0.5)
            br, kr = pb()
            for c in range(4):
                P.op("dve", (lambda c: lambda h: h.tensor_scalar_mul(out=Dg.ap, in0=ID_f, scalar1=RS.ap[:, c:c + 1]))(c),
                     RS.k() + ["cmat"], Dg.k())
                MM(br[:, c * 128:(c + 1) * 128], ONES_f, Dg.ap, True, True, Dg.k() + ["cmat"], [kr])
            CP("act", RBc.ap, br[:], [kr], RBc.k())
            for mg in range(4):
                banks = [pb() for _ in range(4)]
                for ib in range(8):
                    unit, ukey = loadB(("ow", mg, ib), ow_src(mg, ib))
                    for il in range(4):
                        i = 4 * ib + il
                        for ml in range(4):
                            MM(banks[ml][0][:], unit[:, il, ml * 128:(ml + 1) * 128], ygT.ap[:, i, :], i == 0, i == 31,
                               [ukey] + ygT.k(i * NT, (i + 1) * NT), [banks[ml][1]])
                for ml in range(4):
                    m = 4 * mg + ml
                    xm = xres[:, m, HALO:]
                    tb = tb2.ap[:, ml % 2, :]
                    tbk = tb2.k((ml % 2) * NT, (ml % 2 + 1) * NT)
                    TT("dve", tb, banks[ml][0][:], RBc.ap, ALU.mult, [banks[ml][1]] + RBc.k(), tbk)
                    STT("dve", xm, tb, INVA, xm, ALU.mult, ALU.add, tbk + [xk(m)], [xk(m)])

        nextp = {}
        for q in range(n_pass):
            t0 = q * NT
            if q == 0:
                for i in range(4):
                    dst = xres[:, 4 * i:4 * i + 4, HALO:]
                    src = xT_v[:, 4 * i:4 * i + 4, 0:NT]
                    P.dma("sp", (lambda dst, src: lambda h: h.dma_start(out=dst, in_=src))(dst, src), chx[i], writes=XK[4 * i:4 * i + 4])

            def st_pool():
                if q == 0:
                    pool_prefetch()
                    pooled, th = pool_front(0, xres[:], lambda a, b: XK[a:b], True)
                    for f in th:
                        f()
                    pool_back(pooled, xres[:], lambda a, b: XK[a:b])
                else:
                    xin = nextp["xin"]
                    pool_back(nextp["pooled"], xin.ap, lambda a, b: xin.k(a * XW, b * XW))
                mlp_prefetch(0)

            def st_mlp0():
                mlp(0)
                ple_prefetch(0)

            def st_ple0():
                ple(0, t0)
                if do_l1 and dbg != "ple0":
                    ssd_prefetch(0)

            def st_ssd():
                ssd_mixer(q)
                mlp_prefetch(1)

            def st_mlp1():
                mlp(1)
                ple_prefetch(1)

            def st_ple1():
                if q + 1 < n_pass and dbg is None:
                    def start():
                        xin = nextp["xin"]
                        pooled, th = pool_front(q + 1, xin.ap, lambda a, b: xin.k(a * XW, b * XW), False)
                        nextp["xin"] = xin
                        nextp["pooled"] = pooled
                        return th
                    ple(1, t0, extra=dict(start=start, th=None))
                    pool_prefetch()
                else:
                    ple(1, t0)
            def st_ln3():
                if q + 1 < n_pass and dbg is None:
                    nb = 16 * XW * 4
                    off = (ARENA_BYTES - nb) // 1024 * 1024
                    v = arena[:, off // 2:(off + nb) // 2].bitcast(F32).rearrange("p (a b) -> p a b", a=16)
                    xin = ABuf(v, off, nb, 4)
                    t1n = (q + 1) * NT
                    for i in range(4):
                        dst = xin.ap[:, 4 * i:4 * i + 4, :]
                        src = xT_v[:, 4 * i:4 * i + 4, t1n - HALO:t1n + NT]
                        P.dma("sp", (lambda dst, src: lambda h: h.dma_start(out=dst, in_=src))(dst, src), chx[i],
                              writes=xin.k(4 * i * XW, (4 * i + 4) * XW))
                    nextp["xin"] = xin
                layer_norm(3)
            stages = [("pool", st_pool), ("ln0", lambda: layer_norm(0)), ("mlp0", st_mlp0),
                      ("ln1", lambda: layer_norm(1)), ("ple0", st_ple0)]
            if do_l1:
                stages += [("ssd", st_ssd), ("ln2", lambda: layer_norm(2)), ("mlp1", st_mlp1),
                           ("ln3", st_ln3), ("ple1", st_ple1)]
            for name, fn in stages:
                fn()
                if dbg == name:
                    break
            store_out(t0)
        P.barrier("sp", ["out%d" % i for i in range(4)])
        with nc.Block() as block:
            P.emit(block)
        stats = P.stats
    return nc, stats


def host_consts(inputs):
    f = np.float32
    colc = np.zeros((128, NCOL), f)

    def cols(v):
        return np.ascontiguousarray(np.asarray(v, f).reshape(-1, 128).T)
    for i in range(2):
        for j in range(2):
            li = i * 2 + j
            colc[:, C_LNG + li * 16:C_LNG + li * 16 + 16] = cols(inputs["ln_g"][i, j])
            colc[:, C_LNB + li * 16:C_LNB + li * 16 + 16] = cols(inputs["ln_b"][i, j])
    colc[:, C_PS:C_PS + 16] = cols(inputs["pool_scale"][0])
    for k in range(4):
        colc[:, C_CW + k * 48:C_CW + (k + 1) * 48] = cols(inputs["ssm_conv_w"][0, k])
    colc[:, C_CB:C_CB + 48] = cols(inputs["ssm_conv_b"][0])
    colc[:, C_NW:C_NW + 32] = cols(inputs["ssm_norm_w"][0])
    rowc = np.zeros((128, 3, 64), f)
    rowc[:, 0, :] = np.asarray(inputs["ssm_dt_bias"][0], f)[None, :]
    rowc[:, 1, :] = np.asarray(inputs["ssm_a_log"][0], f)[None, :]
    rowc[:, 2, :] = np.asarray(inputs["ssm_d"][0], f)[None, :]
    j = np.arange(128)
    cmat = np.zeros((128, 4, 128), f)
    cmat[:, 0, :] = (j[:, None] <= j[None, :])
    cmat[:, 1, :] = (j[:, None] > j[None, :])
    cmat[:, 2, :] = 1.0
    cmat[:, 3, :] = np.eye(128, dtype=f)
    invc = np.zeros((128, 4, 16), f)
    for g, w in enumerate(WINS):
        invc[:, g, :] = 1.0 / np.minimum(np.arange(16) + 1, w).astype(f)
    return colc, rowc.reshape(128, 192), cmat.reshape(128, 512), invc.reshape(128, 64)


def make_in_maps(inputs, cores):
    f = np.float32
    colc, rowc, cmat, invc = host_consts(inputs)
    def slabs(w, ns):
        return np.ascontiguousarray(w.reshape(16, 128, ns, 256).transpose(2, 1, 0, 3)).reshape(ns, 128, 4096)

    def units(w, nj):
        return np.ascontiguousarray(w.reshape(nj, 4, 128, 4, 512).transpose(3, 0, 2, 1, 4)).reshape(4, nj, 128, 2048)
    inw = np.asarray(inputs["ssm_in_w"], f)[0]
    inr = np.empty((8, 5, 128, 4096), f)
    for g in range(8):
        blocks = [inw[:, DIN + g * 512:DIN + g * 512 + 256], inw[:, DIN + g * 512 + 256:DIN + (g + 1) * 512],
                  np.concatenate([inw[:, 2 * DIN + g * 128:2 * DIN + (g + 1) * 128],
                                  inw[:, 2 * DIN + 1024 + g * 128:2 * DIN + 1024 + (g + 1) * 128]], axis=1),
                  inw[:, g * 512:g * 512 + 256], inw[:, g * 512 + 256:(g + 1) * 512]]
        for j, blk in enumerate(blocks):
            inr[g, j] = blk.reshape(16, 128, 256).transpose(1, 0, 2).reshape(128, 4096)
    w1 = np.asarray(inputs["mlp_w1"], f)
    w2 = np.asarray(inputs["mlp_w2"], f)
    gw = np.asarray(inputs["ple_gate_w"], f)
    shared = {
        "pool_w": slabs(np.asarray(inputs["pool_w"], f).reshape(2048, 512), 2),
        "in_w": inr,
        "wdt_in": np.ascontiguousarray(inw[:, 10240:10304].reshape(16, 128, 64).transpose(1, 0, 2)).reshape(128, 1024),
        "out_w": units(np.asarray(inputs["ssm_out_w"], f)[0], 8),
        "w1": np.stack([slabs(w1[i], 32) for i in range(2)]),
        "w2": np.stack([units(w2[i], 16) for i in range(2)]),
        "ple_w": np.ascontiguousarray(np.asarray(inputs["ple_w"], f)),
        "gate_w": np.stack([slabs(gw[i], 8) for i in range(2)]),
        "colc": colc, "rowc": rowc, "cmat": cmat, "invc": invc,
    }
    x = np.asarray(inputs["x"], f)
    p = np.asarray(inputs["p"], f)
    maps = []
    for b in cores:
        m = dict(shared)
        m["xT"] = np.ascontiguousarray(x[b].T)
        m["pT"] = np.ascontiguousarray(np.transpose(p[:, b], (0, 2, 1)))
        maps.append(m)
    return maps


_CACHE = {}


def kernel(**inputs):
    if "nc" not in _CACHE:
        _CACHE["nc"] = build_program()[0]
    nc = _CACHE["nc"]
    maps = make_in_maps(inputs, list(range(8)))
    res = run_bass_kernel_spmd(nc, maps, core_ids=list(range(8)))
    out = np.stack([np.ascontiguousarray(r["outT"].T) for r in res.results], axis=0)
    return out.astype(np.float32)
```
